# Optimizing a Trainium2 kernel written in Bass

```python
import math
import jax, jax.numpy as jnp
from jax import lax
import numpy as np

D_MODEL = 1024
BATCH = 8
SEQ = 2048
DEPTH = 2

CHUNK = 64
N_MIXERS = 2
N_ATTN_LAYERS = (DEPTH + 1) // 2
N_GMLP_LAYERS = DEPTH // 2

N_HEADS = 8
HEAD_DIM_QK = D_MODEL // (2 * N_HEADS)
HEAD_DIM_V = 2 * HEAD_DIM_QK
QK_WIDTH = N_HEADS * 2 * HEAD_DIM_QK
V_WIDTH = N_HEADS * HEAD_DIM_V
Q_BLOCK = 128

GMLP_BLOCK = 128
GMLP_HALF = 2 * D_MODEL
GMLP_GROUPS = 8
GMLP_GROUP_DIM = GMLP_HALF // GMLP_GROUPS

D_FF = -(-8 * D_MODEL // (3 * 256)) * 256

DEEPNORM_ALPHA = (2 * DEPTH) ** 0.25
DEEPNORM_BETA = (8 * DEPTH) ** -0.25
LN_EPS = 1e-5
MASK_VALUE = -1e30

kernel_name = "hybrid_diffattn_gmlp_deepnorm"


def _layernorm(x, g, b):
    xf = x.astype(jnp.float32)
    mu = jnp.mean(xf, axis=-1, keepdims=True)
    var = jnp.mean(jnp.square(xf - mu), axis=-1, keepdims=True)
    y = (xf - mu) * lax.rsqrt(var + LN_EPS) * g.astype(jnp.float32) + b.astype(jnp.float32)
    return y.astype(x.dtype)


def _rmsnorm(x, g):
    xf = x.astype(jnp.float32)
    y = xf * lax.rsqrt(jnp.mean(jnp.square(xf), axis=-1, keepdims=True) + LN_EPS) * g.astype(jnp.float32)
    return y.astype(x.dtype)


def _lambda_init(layer_idx):
    return 0.8 - 0.6 * math.exp(-0.3 * layer_idx)


def _diff_attention(h, w_qkv, lq1, lk1, lq2, lk2, g_sub, w_o, lambda_init, slopes, pos):
    B, S, _ = h.shape
    qkv = h @ w_qkv
    q, k, v = jnp.split(qkv, [QK_WIDTH, 2 * QK_WIDTH], axis=-1)
    q = q.reshape(B, S, N_HEADS, 2, HEAD_DIM_QK)
    k = k.reshape(B, S, N_HEADS, 2, HEAD_DIM_QK)
    v = v.reshape(B, S, N_HEADS, HEAD_DIM_V)
    lam = (jnp.exp(jnp.sum(lq1.astype(jnp.float32) * lk1.astype(jnp.float32)))
           - jnp.exp(jnp.sum(lq2.astype(jnp.float32) * lk2.astype(jnp.float32)))
           + lambda_init)
    scale = HEAD_DIM_QK ** -0.5
    n_blk = S // Q_BLOCK
    q_blocks = q.reshape(B, n_blk, Q_BLOCK, N_HEADS, 2, HEAD_DIM_QK).transpose(1, 0, 2, 3, 4, 5)
    qpos_blocks = pos.reshape(n_blk, Q_BLOCK)
    k_chunk = pos // CHUNK

    def one_block(args):
        q_blk, qpos = args
        s = jnp.einsum('bqhcd,bkhcd->bhcqk', q_blk, k).astype(jnp.float32) * scale
        dist = jnp.abs(qpos[:, None] - pos[None, :]).astype(jnp.float32)
        s = s - slopes[None, :, None, None, None] * dist[None, None, None]
        allowed = k_chunk[None, :] <= (qpos // CHUNK)[:, None]
        s = jnp.where(allowed[None, None, None], s, MASK_VALUE)
        p = jax.nn.softmax(s, axis=-1)
        a = (p[:, :, 0] - lam * p[:, :, 1]).astype(v.dtype)
        return jnp.einsum('bhqk,bkhd->bqhd', a, v)

    o = lax.map(one_block, (q_blocks, qpos_blocks))
    o = o.transpose(1, 0, 2, 3, 4).reshape(B, S, N_HEADS, HEAD_DIM_V)
    o = _rmsnorm(o, g_sub) * (1.0 - lambda_init)
    return o.reshape(B, S, V_WIDTH) @ w_o


def _gmlp_spatial_gating(h, w_in, b_in, ln_g, ln_b, w_s, b_s, w_out):
    B, S, _ = h.shape
    z = jax.nn.gelu(h @ w_in + b_in, approximate=False)
    u, v = jnp.split(z, 2, axis=-1)
    v = _layernorm(v, ln_g, ln_b)
    v = v.reshape(B, S // GMLP_BLOCK, GMLP_BLOCK, GMLP_GROUPS, GMLP_GROUP_DIM)
    tri = jnp.tril(jnp.ones((GMLP_BLOCK, GMLP_BLOCK), dtype=w_s.dtype))
    v = jnp.einsum('gts,bnsgc->bntgc', w_s * tri[None], v) + b_s.T[None, None, :, :, None]
    y = u * v.reshape(B, S, GMLP_HALF)
    return y @ w_out


def _swiglu(h, w_gate, w_up, w_down):
    return (jax.nn.silu(h @ w_gate) * (h @ w_up)) @ w_down


def setup_inputs(seed: int = 0) -> dict:
    key = jax.random.key(seed)
    ks = jax.random.split(key, 24)
    f32 = jnp.float32

    def nrm(k, shape, scale):
        return jax.random.normal(k, shape, dtype=f32) * scale

    x = nrm(ks[0], (BATCH, SEQ, D_MODEL), 1.0)
    w_qk = nrm(ks[1], (N_ATTN_LAYERS, D_MODEL, 2 * QK_WIDTH), D_MODEL ** -0.5)
    w_v = nrm(ks[2], (N_ATTN_LAYERS, D_MODEL, V_WIDTH), D_MODEL ** -0.5 * DEEPNORM_BETA)
    attn_w_qkv = jnp.concatenate([w_qk, w_v], axis=-1)
    attn_lambda_q1 = nrm(ks[3], (N_ATTN_LAYERS, HEAD_DIM_QK), 0.1)
    attn_lambda_k1 = nrm(ks[4], (N_ATTN_LAYERS, HEAD_DIM_QK), 0.1)
    attn_lambda_q2 = nrm(ks[5], (N_ATTN_LAYERS, HEAD_DIM_QK), 0.1)
    attn_lambda_k2 = nrm(ks[6], (N_ATTN_LAYERS, HEAD_DIM_QK), 0.1)
    attn_subln_g = 1.0 + nrm(ks[7], (N_ATTN_LAYERS, HEAD_DIM_V), 0.02)
    attn_w_o = nrm(ks[8], (N_ATTN_LAYERS, V_WIDTH, D_MODEL), V_WIDTH ** -0.5 * DEEPNORM_BETA)

    gmlp_w_in = nrm(ks[9], (N_GMLP_LAYERS, D_MODEL, 2 * GMLP_HALF), D_MODEL ** -0.5)
    gmlp_b_in = nrm(ks[10], (N_GMLP_LAYERS, 2 * GMLP_HALF), 0.01)
    gmlp_ln_g = 1.0 + nrm(ks[11], (N_GMLP_LAYERS, GMLP_HALF), 0.02)
    gmlp_ln_b = nrm(ks[12], (N_GMLP_LAYERS, GMLP_HALF), 0.01)
    gmlp_w_s = nrm(ks[13], (N_GMLP_LAYERS, GMLP_GROUPS, GMLP_BLOCK, GMLP_BLOCK), GMLP_BLOCK ** -0.5)
    gmlp_b_s = 1.0 + nrm(ks[14], (N_GMLP_LAYERS, GMLP_GROUPS, GMLP_BLOCK), 0.01)
    gmlp_w_out = nrm(ks[15], (N_GMLP_LAYERS, GMLP_HALF, D_MODEL), GMLP_HALF ** -0.5 * DEEPNORM_BETA)

    ln_mix_g = 1.0 + nrm(ks[16], (DEPTH, D_MODEL), 0.02)
    ln_mix_b = nrm(ks[17], (DEPTH, D_MODEL), 0.01)
    ffn_w_gate = nrm(ks[18], (DEPTH, D_MODEL, D_FF), D_MODEL ** -0.5)
    ffn_w_up = nrm(ks[19], (DEPTH, D_MODEL, D_FF), D_MODEL ** -0.5 * DEEPNORM_BETA)
    ffn_w_down = nrm(ks[20], (DEPTH, D_FF, D_MODEL), D_FF ** -0.5 * DEEPNORM_BETA)
    ln_ffn_g = 1.0 + nrm(ks[21], (DEPTH, D_MODEL), 0.02)
    ln_ffn_b = nrm(ks[22], (DEPTH, D_MODEL), 0.01)
    return {
        "x": x,
        "attn_w_qkv": attn_w_qkv,
        "attn_lambda_q1": attn_lambda_q1,
        "attn_lambda_k1": attn_lambda_k1,
        "attn_lambda_q2": attn_lambda_q2,
        "attn_lambda_k2": attn_lambda_k2,
        "attn_subln_g": attn_subln_g,
        "attn_w_o": attn_w_o,
        "gmlp_w_in": gmlp_w_in,
        "gmlp_b_in": gmlp_b_in,
        "gmlp_ln_g": gmlp_ln_g,
        "gmlp_ln_b": gmlp_ln_b,
        "gmlp_w_s": gmlp_w_s,
        "gmlp_b_s": gmlp_b_s,
        "gmlp_w_out": gmlp_w_out,
        "ln_mix_g": ln_mix_g,
        "ln_mix_b": ln_mix_b,
        "ffn_w_gate": ffn_w_gate,
        "ffn_w_up": ffn_w_up,
        "ffn_w_down": ffn_w_down,
        "ln_ffn_g": ln_ffn_g,
        "ln_ffn_b": ln_ffn_b,
    }


def reference(x, attn_w_qkv, attn_lambda_q1, attn_lambda_k1, attn_lambda_q2, attn_lambda_k2,
              attn_subln_g, attn_w_o, gmlp_w_in, gmlp_b_in, gmlp_ln_g, gmlp_ln_b, gmlp_w_s,
              gmlp_b_s, gmlp_w_out, ln_mix_g, ln_mix_b, ffn_w_gate, ffn_w_up, ffn_w_down,
              ln_ffn_g, ln_ffn_b):
    S = x.shape[1]
    pos = jnp.arange(S, dtype=jnp.int32)
    slopes = 2.0 ** (-8.0 * jnp.arange(1, N_HEADS + 1, dtype=jnp.float32) / N_HEADS)
    for i in range(DEPTH):
        j = i // N_MIXERS
        if i % N_MIXERS == 0:
            mix = _diff_attention(x, attn_w_qkv[j], attn_lambda_q1[j], attn_lambda_k1[j],
                                  attn_lambda_q2[j], attn_lambda_k2[j], attn_subln_g[j],
                                  attn_w_o[j], _lambda_init(i), slopes, pos)
        else:
            mix = _gmlp_spatial_gating(x, gmlp_w_in[j], gmlp_b_in[j], gmlp_ln_g[j], gmlp_ln_b[j],
                                       gmlp_w_s[j], gmlp_b_s[j], gmlp_w_out[j])
        x = _layernorm(DEEPNORM_ALPHA * x + mix, ln_mix_g[i], ln_mix_b[i])
        x = _layernorm(DEEPNORM_ALPHA * x + _swiglu(x, ffn_w_gate[i], ffn_w_up[i], ffn_w_down[i]),
                       ln_ffn_g[i], ln_ffn_b[i])
    return x
```

```python
import math
import numpy as np
import ml_dtypes
import concourse.bass as bass
import concourse.mybir as mybir
from contextlib import ExitStack
from concourse.bass_utils import run_bass_kernel_spmd

F32 = mybir.dt.float32
BF16 = mybir.dt.bfloat16
AF = mybir.ActivationFunctionType
ALU = mybir.AluOpType

ENGS = ("pe", "act", "dve", "pool", "sp")
NDSEM = 24

D = 1024
H = 8
DFF = 2816
GH = 2048
ALPHA = 4.0 ** 0.25
EPS = 1e-5
LAMBDA_INIT0 = 0.8 - 0.6 * math.exp(0.0)
SLOPES = [2.0 ** (-(h + 1)) for h in range(H)]
NEG_BIG = -1.0e5


class Buf:
    __slots__ = ("w", "r")

    def __init__(self):
        self.w = None
        self.r = {}


class Prog:
    def __init__(self):
        self.ins = []
        self.streams = {e: [] for e in ENGS}
        self.ndma = 0
        self.dma_since_fence = []

    def op(self, eng, fn, reads=(), writes=(), dma=False, extra=None):
        i = len(self.ins)
        deps = {}
        for b in reads:
            if b.w is not None:
                deps[b.w] = "RAW"
        for b in writes:
            if b.w is not None:
                deps.setdefault(b.w, "WAW")
            for rid in b.r.values():
                deps.setdefault(rid, "WAR")
        if extra:
            for d in extra:
                deps[d] = "RAW"
        deps.pop(i, None)
        for b in reads:
            if dma:
                b.r[("dma", i)] = i
            else:
                b.r[eng] = i
        for b in writes:
            b.w = i
            b.r = {}
        rec = dict(eng=eng, fn=fn, deps=deps, dma=dma, sig=False, cnt=None, dsem=None, dval=None)
        if dma:
            rec["dsem"] = self.ndma % NDSEM
            rec["dval"] = 16 * (self.ndma // NDSEM + 1)
            self.ndma += 1
            self.dma_since_fence.append(i)
        self.ins.append(rec)
        self.streams[eng].append(i)
        return i

    def fence(self):
        last = []
        for e in ENGS:
            for i in reversed(self.streams[e]):
                if self.ins[i]["fn"] is not None and not self.ins[i]["dma"]:
                    last.append(i)
                    break
        deps = last + self.dma_since_fence
        self.dma_since_fence = []
        for e in ENGS:
            self.op(e, None, extra=deps)

    def emit(self, nc, stack):
        ins = self.ins
        for i, rec in enumerate(ins):
            keep = {}
            for d, kind in rec["deps"].items():
                p = ins[d]
                if p["dma"]:
                    keep[d] = kind
                    continue
                if p["eng"] == rec["eng"] and not rec["dma"]:
                    if rec["eng"] == "pe" or kind != "RAW":
                        continue
                keep[d] = kind
                p["sig"] = True
            rec["deps"] = keep
        cnt = {e: 0 for e in ENGS}
        for e in ENGS:
            for i in self.streams[e]:
                rec = ins[i]
                if rec["sig"] and not rec["dma"]:
                    cnt[e] += 1
                    rec["cnt"] = cnt[e]
        esem = {e: stack.enter_context(nc.semaphore("s_" + e)) for e in ENGS}
        dsem = [stack.enter_context(nc.semaphore("d_%d" % k)) for k in range(NDSEM)]
        block = stack.enter_context(nc.Block())

        def run_stream(ename, eobj):
            seen = {}
            for i in self.streams[ename]:
                rec = ins[i]
                waits = {}
                for d in rec["deps"]:
                    p = ins[d]
                    if p["dma"]:
                        key = ("d", p["dsem"])
                        val = p["dval"]
                    else:
                        key = ("e", p["eng"])
                        val = p["cnt"]
                    if waits.get(key, 0) < val:
                        waits[key] = val
                if rec["dma"] and rec["dval"] > 16:
                    key = ("d", rec["dsem"])
                    if waits.get(key, 0) < rec["dval"] - 16:
                        waits[key] = rec["dval"] - 16
                for key, val in waits.items():
                    if seen.get(key, 0) >= val:
                        continue
                    seen[key] = val
                    sem = dsem[key[1]] if key[0] == "d" else esem[key[1]]
                    eobj.wait_ge(sem, val)
                if rec["fn"] is None:
                    continue
                r = rec["fn"](eobj)
                if rec["dma"]:
                    r.then_inc(dsem[rec["dsem"]], 16)
                elif rec["sig"]:
                    r.then_inc(esem[ename], 1)

        @block.tensor
        def _(e):
            run_stream("pe", e)

        @block.scalar
        def _(e):
            run_stream("act", e)

        @block.vector
        def _(e):
            run_stream("dve", e)

        @block.gpsimd
        def _(e):
            run_stream("pool", e)

        @block.sync
        def _(e):
            run_stream("sp", e)


def build(S, stop_after=None):
    nc = bass.Bass("TRN2", target_bir_lowering=False)
    NT = S // 128
    P = Prog()

    def din(name, shape, dt=F32):
        return nc.dram_tensor(name, list(shape), dt, kind="ExternalInput").ap()

    x_tm = din("x_tm", [S, D])
    x_fm = din("x_fm", [D, S])
    wqkv = din("wqkv", [D, 3 * D])
    wo_d = din("wo", [D, D])
    win_d = din("win", [D, 2 * GH])
    wout_d = din("wout", [GH, D])
    wg_d = [din("wg%d" % l, [22, 128, D]) for l in range(2)]
    wu_d = [din("wu%d" % l, [22, 128, D]) for l in range(2)]
    wd_d = [din("wd%d" % l, [DFF, D]) for l in range(2)]
    lam_d = din("lamv", [128, 256])
    gsub_d = din("gsub", [128, 128])
    binv_d = din("binv", [128, GH])
    cols_d = din("cols", [128, 48])
    wsT_d = din("wsT", [128, 8, 128])
    bsb_d = din("bsb", [128, 8, 128])
    lnp_d = din("lnp", [8, 128, D])
    ident_d = din("ident", [128, 128], BF16)
    btall_d = din("btall", [128, 5, 512])
    trilT_d = din("trilT", [128, 128])
    out_d = nc.dram_tensor("out", [S, D], F32, kind="ExternalOutput").ap()

    with ExitStack() as st:
        ARENA_B = 208896
        arena = st.enter_context(nc.sbuf_tensor("arena", [128, ARENA_B // 2], BF16))

        def view(off, shape, dt):
            n = 1
            for s_ in shape:
                n *= s_
            nb = n * (4 if dt == F32 else 2)
            assert off % 4 == 0 and off + nb <= ARENA_B, (off, nb)
            ap = arena[:, off // 2:(off + nb) // 2]
            if dt == F32:
                ap = ap.bitcast(F32)
            if len(shape) == 2:
                ap = ap.rearrange("p (a b) -> p a b", a=shape[0])
            elif len(shape) == 3:
                ap = ap.rearrange("p (a b c) -> p a b c", a=shape[0], b=shape[1])
            return ap

        PSA = st.enter_context(nc.psum_tensor("psa", [128, 3584], F32))
        banks = [PSA[:, i * 512:(i + 1) * 512] for i in range(7)]
        bankT = st.enter_context(nc.psum_tensor("pbT", [128, 1024], BF16))
        bb = [Buf() for _ in range(7)]
        bbT = Buf()

        xres = view(0, [NT, D], F32)
        xT = view(65536, [8, S], BF16)
        gam = view(98304, [D], F32)
        bet = view(102400, [D], F32)
        xb = view(106496, [2, D], BF16)
        ident = view(110592, [128], BF16)
        ones_bf = view(110848, [128], BF16)
        smalls = view(111104, [256], F32)
        cols = view(112128, [48], F32)
        lamt = view(112128 + 192, [256], F32)
        lamtmp = view(112128 + 192 + 1024, [128], F32)
        PH = 114176
        b_xres = [Buf() for _ in range(NT)]
        b_xT = [Buf() for _ in range(NT)]
        b_gb = Buf()
        b_xb = [Buf(), Buf()]
        b_ident = Buf()
        b_ones = Buf()
        b_cols = Buf()
        b_lam = Buf()
        b_sm = {}

        def smb(key):
            if key not in b_sm:
                b_sm[key] = Buf()
            return b_sm[key]

        def MM(out, lhsT, rhs, start, stop, R, W, sgc=False):
            P.op("pe", lambda e: e.matmul(out, lhsT=lhsT, rhs=rhs, start=start, stop=stop, skip_group_check=sgc), reads=R, writes=W)

        def TR(out, in_, R, W):
            P.op("pe", lambda e: e.transpose(out=out, in_=in_, identity=ident), reads=list(R) + [b_ident], writes=W)

        def ACTV(out, in_, func, R, W, **kw):
            P.op("act", lambda e: e.activation(out=out, in_=in_, func=func, **kw), reads=R, writes=W)

        def STT(out, in0, scalar, in1, op0, op1, R, W):
            P.op("dve", lambda e: e.scalar_tensor_tensor(out=out, in0=in0, scalar=scalar, in1=in1, op0=op0, op1=op1), reads=R, writes=W)

        def TT(eng, out, in0, in1, op, R, W):
            P.op(eng, lambda e: e.tensor_tensor(out=out, in0=in0, in1=in1, op=op), reads=R, writes=W)

        def TS(eng, out, in0, s1, s2, op0, op1, R, W):
            if op1 is None:
                P.op(eng, lambda e: e.tensor_scalar(out=out, in0=in0, scalar1=s1, scalar2=None, op0=op0), reads=R, writes=W)
            else:
                P.op(eng, lambda e: e.tensor_scalar(out=out, in0=in0, scalar1=s1, scalar2=s2, op0=op0, op1=op1), reads=R, writes=W)

        def CP(eng, out, in_, R, W):
            if eng == "act":
                ACTV(out, in_, AF.Copy, R, W)
            else:
                P.op(eng, lambda e: e.tensor_copy(out=out, in_=in_), reads=R, writes=W)

        def DMA(q, out, in_, R, W):
            return P.op(q, lambda e: e.dma_start(out=out, in_=in_), reads=R, writes=W, dma=True)

        xv = x_tm.rearrange("(t p) d -> p t d", p=128)
        xfv = x_fm.rearrange("(kc p) s -> p kc s", p=128)
        def load_xT(tt):
            DMA("pool", xT[:, :, tt * 512:(tt + 1) * 512], xfv[:, :, tt * 512:(tt + 1) * 512], [], b_xT[tt * 4:tt * 4 + 4])

        load_xT(0)

        def load_xres():
            for t0 in range(0, NT, 4):
                DMA("sp", xres[:, t0:t0 + 4, :], xv[:, t0:t0 + 4, :], [], b_xres[t0:t0 + 4])
        DMA("sp", ident, ident_d, [], [b_ident])
        DMA("sp", cols, cols_d, [], [b_cols])
        DMA("sp", lamt, lam_d, [], [b_lam])
        P.op("pool", lambda e: e.memset(ones_bf, 1.0), writes=[b_ones])
        negh = smalls[:, 0:1]
        P.op("pool", lambda e: e.memset(negh, -0.5), writes=[smb("negh")])
        epsc = smalls[:, 7:8]
        P.op("pool", lambda e: e.memset(epsc, EPS), writes=[smb("epsc")])

        TT("dve", lamtmp, lamt[:, 0:128], lamt[:, 128:256], ALU.mult, [b_lam], [smb("lamtmp")])
        s12 = smalls[:, 1:3]
        P.op("dve", lambda e: e.tensor_reduce(out=s12, in_=lamtmp.rearrange("p (a b) -> p a b", a=2), axis=mybir.AxisListType.X, op=ALU.add),
             reads=[smb("lamtmp")], writes=[smb("s12")])
        e12 = smalls[:, 3:5]
        ACTV(e12, s12, AF.Exp, [smb("s12")], [smb("e12")])
        lamd = smalls[:, 5:6]
        TT("dve", lamd, e12[:, 0:1], e12[:, 1:2], ALU.subtract, [smb("e12")], [smb("lamd")])
        neglam = smalls[:, 6:7]
        TS("dve", neglam, lamd, LAMBDA_INIT0, -1.0, ALU.add, ALU.mult, [smb("lamd")], [smb("neglam")])

        SM_LN = 8
        SM_AT = 48
        SM_GV = 64

        def load_ln(idx_g, idx_b):
            DMA("sp", gam, lnp_d[idx_g], [], [b_gb])
            DMA("sp", bet, lnp_d[idx_b], [], [b_gb])

        ln_ctr = [0]

        def ln_tail(ti, last=False):
            k = ln_ctr[0] % 2
            ln_ctr[0] += 1
            base = SM_LN + 20 * k
            stt = smalls[:, base:base + 12]
            mv = smalls[:, base + 12:base + 14]
            ve = smalls[:, base + 14:base + 15]
            rstd = smalls[:, base + 15:base + 16]
            nmr = smalls[:, base + 16:base + 17]
            bs = smb(("ln", k))
            r = xres[:, ti, :]
            for hf in range(2):
                P.op("dve", lambda e, hf=hf: e.bn_stats(out=stt[:, hf * 6:hf * 6 + 6], in_=xres[:, ti, hf * 512:(hf + 1) * 512]),
                     reads=[b_xres[ti]], writes=[bs])
            P.op("dve", lambda e: e.bn_aggr(out=mv, in_=stt), reads=[bs], writes=[bs])
            TS("dve", ve, mv[:, 1:2], EPS, None, ALU.add, None, [bs], [bs])
            TT("pool", rstd, ve, negh, ALU.pow, [bs, smb("negh")], [bs])
            def stage_a2():
                STT(nmr, mv[:, 0:1], -1.0, rstd, ALU.mult, ALU.mult, [bs], [bs])
                ACTV(r, r, AF.Identity, [bs, b_xres[ti]], [b_xres[ti]], scale=rstd, bias=nmr)
                TT("pool", r, r, gam, ALU.mult, [b_xres[ti], b_gb], [b_xres[ti]])

            def stage_b():
                TT("dve", r, r, bet, ALU.add, [b_xres[ti], b_gb], [b_xres[ti]])
                if not last:
                    ACTV(xb[:, k, :], r, AF.Copy, [b_xres[ti]], [b_xb[k]])
                else:
                    DMA("sp", out_d[ti * 128:(ti + 1) * 128, :], xres[:, ti, :], [b_xres[ti]], [])

            def stage_c():
                if last:
                    return
                for kc in range(8):
                    TR(bankT[:, kc * 128:(kc + 1) * 128], xb[:, k, kc * 128:(kc + 1) * 128], [b_xb[k]], [bbT])
                CP("act", xT[:, :, ti * 128:(ti + 1) * 128], bankT.ap().rearrange("p (a b) -> p a b", a=8), [bbT], [b_xT[ti]])
            return [stage_a2, stage_b, stage_c]

        pend_fin = []

        def ln_step(ti, last=False):
            l1 = [e_ for e_ in pend_fin if len(e_) == 1]
            l2 = [e_ for e_ in pend_fin if len(e_) == 2]
            l3 = [e_ for e_ in pend_fin if len(e_) == 3]
            for e_ in l1:
                e_.pop(0)()
            ent_new = ln_tail(ti, last=last)
            for e_ in l3:
                e_.pop(0)()
            for e_ in l2:
                e_.pop(0)()
            for e_ in list(pend_fin):
                if not e_:
                    pend_fin.remove(e_)
            pend_fin.append(ent_new)

        def flush_fin(keep=0):
            while pend_fin:
                for ent in list(pend_fin):
                    ent.pop(0)()
                    if not ent:
                        pend_fin.remove(ent)
                if keep:
                    break

        o = PH
        qT = [view(o + i * 2 * S, [S], BF16) for i in range(2)]; o += 4 * S
        kT = [view(o + i * 2 * S, [S], BF16) for i in range(2)]; o += 4 * S
        VA_B = ((NT * 129 * 2 + 3) // 4) * 4
        Vaug = [view(o + i * VA_B, [NT, 129], BF16) for i in range(2)]; o += 2 * VA_B
        wqk1 = [view(o + j * 2048, [8, 128], BF16) for j in range(3)]; o += 3 * 2048
        wqk = [wqk1, wqk1]
        Sf = [view(o + i * 4096, [2, 512], F32) for i in range(2)]; o += 8192
        oT = view(o, [8, S], BF16); o_oT = o; o += 16 * S
        Pt = [view(o + i * 2048, [2, 512], BF16) for i in range(3)]; o += 6144
        btall = view(o, [5, 512], F32); o += 10240
        gst = view(o, [128], F32); o += 512
        t1t = [view(o + i * 512, [128], F32) for i in range(2)]; o += 1024
        at = [view(o + i * 512, [128], F32) for i in range(2)]; o += 1024
        obt = [view(o + i * 256, [128], BF16) for i in range(2)]; o += 512
        junk = view(o, [128], F32); o += 512
        Osb = [view(o + i * 1040, [2, 130], F32) for i in range(2)]; o += 2080
        b_Osb = [Buf(), Buf()]
        assert o <= ARENA_B, o
        wo_sb = view(PH, [8, D], BF16)
        assert PH + 16384 <= o_oT

        b_q = [Buf(), Buf()]; b_k = [Buf(), Buf()]; b_v = [Buf(), Buf()]
        b_w1 = [Buf() for _ in range(3)]
        b_w = [b_w1, b_w1]
        b_Sf = [Buf(), Buf()]
        b_oT = [Buf() for _ in range(NT)]
        b_Pt = [Buf() for _ in range(3)]
        b_bt = Buf(); b_gs = Buf()
        b_t1 = [Buf(), Buf()]; b_a = [Buf(), Buf()]; b_ob = [Buf(), Buf()]; b_junk = Buf()

        DMA("sp", btall, btall_d, [], [b_bt])
        DMA("sp", gst, gsub_d, [], [b_gs])
        TS("dve", gst, gst, 1.0 - LAMBDA_INIT0, None, ALU.mult, None, [b_gs], [b_gs])
        for i in range(2):
            P.op("pool", lambda e, i=i: e.memset(Vaug[i][:, :, 128:129], 1.0), writes=[b_v[i]])
        load_ln(0, 1)

        wv3 = wqkv.rearrange("(kc p) n -> p kc n", p=128)
        misc_rot = [0]

        def misc_bank():
            return 6

        def proj_items(h):
            hb = h % 2
            items = []

            def ld():
                for j in range(3):
                    c0 = j * D + h * 128
                    DMA("pool", wqk[hb][j], wv3[:, :, c0:c0 + 128], [], [b_w[hb][j]])
            items.append(ld)
            for j, (dst, bd) in enumerate(((qT, b_q), (kT, b_k))):
                for tt in range(S // 512):
                    def f(j=j, dst=dst, bd=bd, tt=tt):
                        bk = misc_bank()
                        for kc in range(8):
                            MM(banks[bk][:, :], wqk[hb][j][:, kc, :], xT[:, kc, tt * 512:(tt + 1) * 512], kc == 0, kc == 7,
                               [b_w[hb][j]] + b_xT[tt * 4:tt * 4 + 4], [bb[bk]])
                        CP("act", dst[hb][:, tt * 512:(tt + 1) * 512], banks[bk][:, :], [bb[bk]], [bd[hb]])
                    items.append(f)
            for vb in range(NT // 4):
                def f(vb=vb):
                    bk = misc_bank()
                    for tl in range(4):
                        ti = vb * 4 + tl
                        for kc in range(8):
                            MM(banks[bk][:, tl * 128:(tl + 1) * 128], xT[:, kc, ti * 128:(ti + 1) * 128], wqk[hb][2][:, kc, :], kc == 0, kc == 7,
                               [b_w[hb][2], b_xT[ti]], [bb[bk]])
                    CP("dve", Vaug[hb][:, vb * 4:vb * 4 + 4, 0:128], banks[bk].rearrange("p (a b) -> p a b", a=4), [bb[bk]], [b_v[hb]])
                items.append(f)
            return items

        s_rot = [0]
        o_rot = [0]
        p_rot = [0]

        Sx = [PSA[:, 0:1024], PSA[:, 1024:2048]]
        bSx = [[bb[0], bb[1]], [bb[2], bb[3]]]

        def attention_head(h, nxt):
            hb = h % 2
            slope = SLOPES[h]
            groups = []
            for qi in range(NT):
                o_rot[0] += 1
                ob = 4 + (o_rot[0] % 2)
                for g in range(qi // 4 + 1):
                    s_rot[0] += 1
                    p_rot[0] += 1
                    groups.append(dict(qi=qi, g=g, ob=ob, sk=s_rot[0] % 2, pk=p_rot[0] % 3, fk=s_rot[0] % 2,
                                       tiles=list(range(4 * g, min(4 * g + 4, qi + 1))), lastq=(g == qi // 4)))

            def scores(G):
                qi, g, sk, pk, tiles = G["qi"], G["g"], G["sk"], G["pk"], G["tiles"]
                n = len(tiles) * 128
                for j, kt in enumerate(tiles):
                    for c in range(2):
                        MM(Sx[sk][:, c * 512 + j * 128:c * 512 + (j + 1) * 128], kT[hb][64 * c:64 * c + 64, kt * 128:(kt + 1) * 128],
                           qT[hb][64 * c:64 * c + 64, qi * 128:(qi + 1) * 128], True, True, [b_k[hb], b_q[hb]], bSx[sk])
                var = (1 + qi - 4 * g) if (4 * g + 3 >= qi) else 0
                sv = Sx[sk].rearrange("p (c n) -> p c n", c=2)[:, :, 0:n]
                fk = G["fk"]
                STT(Sf[fk][:, :, 0:n], btall[:, var, 0:n].unsqueeze(1).broadcast_to([128, 2, n]), 8.0 * slope, sv, ALU.mult, ALU.add,
                    [b_bt] + bSx[sk], [b_Sf[fk]])
                ACTV(Pt[pk][:, :, 0:n], Sf[fk][:, :, 0:n], AF.Exp, [b_Sf[fk]], [b_Pt[pk]], scale=0.125, bias=float(-slope * 128.0 * (qi - 4 * g)))

            def av(G):
                qi, pk, tiles, ob = G["qi"], G["pk"], G["tiles"], G["ob"]
                for j, kt in enumerate(tiles):
                    for c in range(2):
                        MM(banks[ob][:, c * 256:c * 256 + 129], Pt[pk][:, c, j * 128:(j + 1) * 128], Vaug[hb][:, kt, :],
                           (kt == 0 and c == 0), kt == qi, [b_Pt[pk], b_v[hb]], [bb[ob]], sgc=True)
                if G["lastq"]:
                    pipe.append(post_stages(qi, ob))

            def post_stages(qi, ob):
                k2 = qi % 2
                base = SM_AT + 8 * k2
                r12 = smalls[:, base:base + 2]; r2l = smalls[:, base + 2:base + 3]
                ss = smalls[:, base + 3:base + 4]; ms = smalls[:, base + 4:base + 5]; rs = smalls[:, base + 5:base + 6]
                bs = smb(("at", k2))
                bs2 = smb(("at2", k2))
                ov = banks[ob].rearrange("p (c n) -> p c n", c=2)
                osb = Osb[k2]

                def s1():
                    ACTV(osb[:, :, 0:129], ov[:, :, 0:129], AF.Copy, [bb[ob]], [b_Osb[k2]])

                def s2():
                    P.op("dve", lambda e: e.reciprocal(out=r12.unsqueeze(2), in_=osb[:, :, 128:129]), reads=[b_Osb[k2]], writes=[bs])
                    TT("dve", r2l, r12[:, 1:2], neglam, ALU.mult, [bs, smb("neglam")], [bs])
                    TS("dve", t1t[k2], osb[:, 0, 0:128], r12[:, 0:1], None, ALU.mult, None, [b_Osb[k2], bs], [b_t1[k2]])
                    STT(at[k2], osb[:, 1, 0:128], r2l, t1t[k2], ALU.mult, ALU.add, [b_Osb[k2], bs, b_t1[k2]], [b_a[k2]])

                def s3():
                    ACTV(junk, at[k2], AF.Square, [b_a[k2]], [b_junk, bs2], accum_out=ss)
                    ACTV(ms, ss, AF.Ln, [bs2, smb("epsc")], [bs2], scale=1.0 / 128.0, bias=epsc)
                    ACTV(rs, ms, AF.Exp, [bs2], [bs2], scale=-0.5)

                def s4():
                    STT(obt[k2], at[k2], rs, gst, ALU.mult, ALU.mult, [b_a[k2], bs2, b_gs], [b_ob[k2]])

                def s5():
                    TR(bankT[:, 0:128], obt[k2], [b_ob[k2]], [bbT])
                    CP("act", oT[:, h, qi * 128:(qi + 1) * 128], bankT[:, 0:128], [bbT], [b_oT[qi]])
                return [s1, s2, s3, s4, s5]

            def pump_posts():
                for ent in list(pipe):
                    ent.pop(0)()
                    if not ent:
                        pipe.remove(ent)

            if h == H - 1 and S == 2048:
                for hh in range(H):
                    DMA("pool", wo_c[hh], wov[:, hh, :], [], [wo_cb[hh]])
            every = max(1, len(groups) // (len(nxt) + 1)) if nxt else 0
            scores(groups[0])
            if len(groups) > 1:
                scores(groups[1])
            for n, G in enumerate(groups):
                if n + 2 < len(groups):
                    scores(groups[n + 2])
                pump_posts()
                av(G)
                if nxt and (n % every == every - 1):
                    nxt.pop(0)()
            if h == H - 1:
                while pipe:
                    pump_posts()
            while nxt:
                nxt.pop(0)()

        pipe = []
        wov = wo_d.rearrange("(kc p) n -> p kc n", p=128)
        b_wo = Buf()
        if S == 2048:
            wo_off = [PH + 0, PH + 2048, PH + 8192, PH + 10240, PH + 16384, PH + 18432, PH + 16384 + 2 * VA_B, PH + 16384 + 2 * VA_B + 2048]
            wo_c = [view(off_, [D], BF16) for off_ in wo_off]
            wo_cb = [b_q[0], b_q[0], b_k[0], b_k[0], b_v[0], b_v[0], b_w1[0], b_w1[1]]
        else:
            wo_c = [wo_sb[:, hh, :] for hh in range(H)]
            wo_cb = [b_wo] * H
        items0 = proj_items(0)
        items0.pop(0)()
        for tt in range(1, S // 512):
            load_xT(tt)
        for f in items0:
            f()
        load_xres()
        for h in range(H):
            attention_head(h, proj_items(h + 1) if h + 1 < H else [])

        P.fence()
        if S != 2048:
            for kc0 in range(0, 8, 2):
                DMA("pool", wo_sb[:, kc0:kc0 + 2, :], wov[:, kc0:kc0 + 2, :], [], [b_wo])
        rot7 = [0]

        def nb7():
            rot7[0] += 1
            return rot7[0] % 7

        for ti in range(NT):
            for hf in range(2):
                bk = nb7()
                for h in range(H):
                    MM(banks[bk][:, :], oT[:, h, ti * 128:(ti + 1) * 128], wo_c[h][:, hf * 512:(hf + 1) * 512], h == 0, h == H - 1,
                       [b_oT[ti], wo_cb[h]], [bb[bk]])
                xs = xres[:, ti, hf * 512:(hf + 1) * 512]
                STT(xs, xs, ALPHA, banks[bk][:, :], ALU.mult, ALU.add, [b_xres[ti], bb[bk]], [b_xres[ti]])
            ln_step(ti)
        flush_fin()
        P.fence()

        NTT = S // 512
        QP = [(0, 6), (6, 6), (12, 5), (17, 5)]
        o = PH
        hT = view(o, [6, S], BF16); o += 12 * S
        wd_sb = [view(o + i * 2048, [D], BF16) for i in range(12)]; o += 12 * 2048
        wg_sb = [view(o + i * 2048, [D], BF16) for i in range(4)]; o += 8192
        wu_sb = [view(o + i * 2048, [D], BF16) for i in range(4)]; o += 8192
        sgt = [view(o + i * 2048, [512], F32) for i in range(2)]; o += 4096
        assert o <= ARENA_B, o
        FFN_END = o

        def ffn(l, last_layer, hook=None):
            b_hT = [[Buf() for _ in range(NTT)] for _ in range(6)]
            b_wd = [Buf() for _ in range(12)]
            b_wg = [Buf() for _ in range(4)]
            b_wu = [Buf() for _ in range(4)]
            b_sg = [Buf(), Buf()]
            load_ln(2 + 4 * l, 3 + 4 * l)
            wgv = wg_d[l].rearrange("f p n -> p f n")
            wuv = wu_d[l].rearrange("f p n -> p f n")
            wdv = wd_d[l].rearrange("(fc p) n -> p fc n", p=128)
            issued = [0]

            def issue_load():
                fc = issued[0]
                if fc >= 22:
                    return
                issued[0] += 1
                DMA("pool", wg_sb[fc % 4], wgv[:, fc, :], [], [b_wg[fc % 4]])
                DMA("pool", wu_sb[fc % 4], wuv[:, fc, :], [], [b_wu[fc % 4]])
                DMA("pool", wd_sb[fc % 12], wdv[:, fc, :], [], [b_wd[fc % 12]])

            gu_rot = [0]
            dn_rot = [0]
            sg_rot = [0]
            for qp, (c0, cn) in enumerate(QP):
                for j in range(cn):
                    fc = c0 + j
                    while issued[0] <= min(fc + 3, 21):
                        issue_load()
                    sl = fc % 4
                    for tt in range(NTT):
                        gu_rot[0] += 1
                        bg = (gu_rot[0] % 2) * 2
                        bu = bg + 1
                        xb_ = b_xT[tt * 4:tt * 4 + 4]
                        rhs_tok = slice(tt * 512, (tt + 1) * 512)
                        for kc in range(8):
                            MM(banks[bg][:, :], wg_sb[sl][:, kc * 128:(kc + 1) * 128], xT[:, kc, rhs_tok], kc == 0, kc == 7,
                               [b_wg[sl]] + xb_, [bb[bg]])
                        for kc in range(8):
                            MM(banks[bu][:, :], wu_sb[sl][:, kc * 128:(kc + 1) * 128], xT[:, kc, rhs_tok], kc == 0, kc == 7,
                               [b_wu[sl]] + xb_, [bb[bu]])
                        sg_rot[0] += 1
                        sk = sg_rot[0] % 2
                        ACTV(sgt[sk], banks[bg][:, :], AF.Silu, [bb[bg]], [b_sg[sk]])
                        TT("dve", hT[:, j, tt * 512:(tt + 1) * 512], sgt[sk], banks[bu][:, :], ALU.mult, [b_sg[sk], bb[bu]], [b_hT[j][tt]])
                if hook is not None and qp == 0:
                    hook()
                lastp = qp == len(QP) - 1
                for ti in range(NT):
                    for hf in range(2):
                        dn_rot[0] += 1
                        bk = 4 + (dn_rot[0] % 3)
                        for j in range(cn):
                            fc = c0 + j
                            MM(banks[bk][:, :], hT[:, j, ti * 128:(ti + 1) * 128], wd_sb[fc % 12][:, hf * 512:(hf + 1) * 512], j == 0, j == cn - 1,
                               [b_hT[j][ti // 4], b_wd[fc % 12]], [bb[bk]])
                        xs = xres[:, ti, hf * 512:(hf + 1) * 512]
                        if qp == 0:
                            STT(xs, xs, ALPHA, banks[bk][:, :], ALU.mult, ALU.add, [b_xres[ti], bb[bk]], [b_xres[ti]])
                        else:
                            TT("dve", xs, xs, banks[bk][:, :], ALU.add, [b_xres[ti], bb[bk]], [b_xres[ti]])
                    if lastp:
                        ln_step(ti, last=last_layer)
                flush_fin()

        TG2 = min(512, S)
        NB = TG2 // 128
        SGT = min(1024, S)
        NSG = S // SGT
        GPS = SGT // TG2
        NVB = GPS * NB
        o = PH
        vraw = view(o, [NVB, GH], BF16); o_vraw = o; o += NVB * GH * 2
        yT = [view(o + i * 4 * TG2 * 2, [4, TG2], BF16) for i in range(2)]; o_yT = o; o += 2 * 4 * TG2 * 2
        win_sb = [view(o + i * 8192, [8, 512], BF16) for i in range(2)]; o += 2 * 8192
        wob_sb = [view(o + i * 8192, [4, D], BF16) for i in range(2)]; o_wob = o; o += 2 * 8192
        binv = [view(o + i * 2048, [512], F32) for i in range(2)]; o_ug = o; o += 4096
        ugt = binv
        bias2 = view(o, [16, 128], F32); o += 8192
        WmT = view(o, [8, 128], BF16); o += 2048
        tmpv = view(o, [512], F32); o_tmpv = o; o += 2048
        t2t = tmpv
        gsm = view(o, [256], F32); o += 1024
        assert o <= ARENA_B, o
        bsb = view(o_ug, [8, 128], F32)
        wsT = view(o_wob + 8192 + 4096, [8, 128], F32)
        trilT = view(o_tmpv, [128], F32)
        b_vraw = [Buf() for _ in range(NVB)]
        b_yT = [Buf(), Buf()]
        b_win = [Buf(), Buf()]
        b_wob = [Buf(), Buf()]
        b_binv = [Buf(), Buf()]; b_bias2 = Buf(); b_WmT = Buf()
        b_tmpv = Buf(); b_ug = b_binv; b_t2 = b_tmpv
        b_prep = Buf()
        assert min(o_wob + 8192 + 4096, o_ug, o_tmpv) >= FFN_END, (o_wob, o_ug, o_tmpv, FFN_END)
        DMA("sp", bsb, bsb_d, [], [b_prep])
        DMA("sp", wsT, wsT_d, [], [b_prep])
        DMA("sp", trilT, trilT_d, [], [b_prep])
        def gmlp_prep_compute():
            TT("dve", WmT, wsT, trilT.unsqueeze(1).broadcast_to([128, 8, 128]), ALU.mult, [b_prep], [b_WmT])
            for g in range(8):
                bk = g // 4
                MM(banks[bk][:, (g % 4) * 128:(g % 4 + 1) * 128], ones_bf, WmT[:, g, :], True, True, [b_ones, b_WmT], [bb[bk]])
            for uc in range(16):
                g = uc // 2
                STT(bias2[:, uc, :], banks[g // 4][:, (g % 4) * 128:(g % 4 + 1) * 128], cols[:, 32 + uc:33 + uc], bsb[:, g, :], ALU.mult, ALU.add,
                    [bb[g // 4], b_cols, b_prep], [b_bias2])

        ffn(0, False, hook=gmlp_prep_compute)
        P.fence()


        winv = win_d.rearrange("(kc p) n -> p kc n", p=128)
        woutv = wout_d.rearrange("(c p) n -> p c n", p=128)
        wl = []
        for sg in range(NSG):
            for vb in range(4):
                wl.append(("win", GH + vb * 512))
            for ub in range(4):
                wl.append(("win", ub * 512))
                wl.append(("wob", ub * 4))
        wstate = {"next": 0, "win": 0, "wob": 0}
        wslot = {}

        def issue_w():
            i = wstate["next"]
            if i >= len(wl):
                return
            wstate["next"] += 1
            kind, a_ = wl[i]
            if kind == "win":
                sl = wstate["win"] % 2
                wstate["win"] += 1
                wslot[i] = sl
                DMA("pool", win_sb[sl], winv[:, :, a_:a_ + 512], [], [b_win[sl]])
            else:
                sl = wstate["wob"] % 2
                wstate["wob"] += 1
                wslot[i] = sl
                DMA("pool", wob_sb[sl], woutv[:, a_:a_ + 4, :], [], [b_wob[sl]])

        issue_w()
        issue_w()
        load_ln(4, 5)
        wi = [0]

        def next_w(ahead):
            i = wi[0]
            wi[0] += 1
            while wstate["next"] <= i + ahead and wstate["next"] < len(wl):
                issue_w()
            return wslot[i]

        rot_a = [0]
        rot_b = [0]
        rot_c = [0]
        for sg in range(NSG):
            tok_sg = sg * SGT
            tib0 = tok_sg // 128
            vst = gsm[:, 0:NVB * 24]
            b_vst = [smb(("vst", t)) for t in range(NVB)]
            for vb in range(4):
                sl = next_w(1)
                bvk = vb % 2
                DMA("sp", binv[bvk], binv_d[:, vb * 512:(vb + 1) * 512], [], [b_binv[bvk]])
                for til in range(NVB):
                    ti = tib0 + til
                    rot_a[0] += 1
                    bk = rot_a[0] % 3
                    for kc in range(8):
                        MM(banks[bk][:, :], xT[:, kc, ti * 128:(ti + 1) * 128], win_sb[sl][:, kc, :], kc == 0, kc == 7,
                           [b_xT[ti], b_win[sl]], [bb[bk]])
                    TT("dve", tmpv, banks[bk][:, :], binv[bvk], ALU.add, [bb[bk], b_binv[bvk]], [b_tmpv])
                    ACTV(vraw[:, til, vb * 512:(vb + 1) * 512], tmpv, AF.Gelu, [b_tmpv], [b_vraw[til]])
                    P.op("dve", lambda e, til=til, vb=vb: e.bn_stats(out=vst[:, til * 24 + vb * 6:til * 24 + vb * 6 + 6],
                                                                  in_=vraw[:, til, vb * 512:(vb + 1) * 512]),
                         reads=[b_vraw[til]], writes=[b_vst[til]])
                flush_fin(keep=1)
            for til in range(NVB):
                base = NVB * 24 + til * 4
                mv = gsm[:, base:base + 2]; ve = gsm[:, base + 2:base + 3]; rstd = gsm[:, base + 3:base + 4]
                bs = b_vst[til]
                P.op("dve", lambda e, til=til, mv=mv: e.bn_aggr(out=mv, in_=vst[:, til * 24:(til + 1) * 24]), reads=[bs], writes=[bs])
                TS("dve", ve, mv[:, 1:2], EPS, None, ALU.add, None, [bs], [bs])
                TT("pool", rstd, ve, negh, ALU.pow, [bs, smb("negh")], [bs])
            for til in range(NVB):
                base = NVB * 24 + til * 4
                mv = gsm[:, base:base + 2]; rstd = gsm[:, base + 3:base + 4]
                bs = b_vst[til]
                TS("dve", vraw[:, til, :], vraw[:, til, :], mv[:, 0:1], rstd, ALU.subtract, ALU.mult, [bs, b_vraw[til]], [b_vraw[til]])
            for ub in range(4):
                sl = next_w(1)
                slo = next_w(1)
                for gi in range(GPS):
                    tok0 = tok_sg + gi * TG2
                    tib = tok0 // 128
                    yk = (ub * GPS + gi) % 2
                    for ucl in range(4):
                        uc = ub * 4 + ucl
                        gg = uc // 2
                        rot_b[0] += 1
                        bu = 3 + (rot_b[0] % 2)
                        for kc in range(8):
                            MM(banks[bu][:, 0:TG2], win_sb[sl][:, kc, ucl * 128:(ucl + 1) * 128], xT[:, kc, tok0:tok0 + TG2], kc == 0, kc == 7,
                               [b_win[sl]] + b_xT[tib:tib + NB], [bb[bu]])
                        k = rot_b[0] % 2
                        ACTV(ugt[k][:, 0:TG2], banks[bu][:, 0:TG2], AF.Gelu, [bb[bu], b_cols], [b_ug[k]], bias=cols[:, uc:uc + 1])
                        bs_ = 5 + (rot_b[0] % 2)
                        for blk in range(NB):
                            vb_ = gi * NB + blk
                            MM(banks[bs_][:, blk * 128:(blk + 1) * 128], vraw[:, vb_, uc * 128:(uc + 1) * 128], WmT[:, gg, :], True, True,
                               [b_vraw[vb_], b_WmT], [bb[bs_]])
                        STT(t2t[:, 0:TG2].rearrange("p (a b) -> p a b", a=NB), banks[bs_][:, 0:TG2].rearrange("p (a b) -> p a b", a=NB),
                            cols[:, 16 + uc:17 + uc], bias2[:, uc, :].unsqueeze(1).broadcast_to([128, NB, 128]), ALU.mult, ALU.add,
                            [bb[bs_], b_cols, b_bias2], [b_t2])
                        TT("pool", yT[yk][:, ucl, :], t2t[:, 0:TG2], ugt[k][:, 0:TG2], ALU.mult, [b_t2, b_ug[k]], [b_yT[yk]])
                    for til in range(NB):
                        ti = tib + til
                        for hf in range(2):
                            rot_c[0] += 1
                            bk = rot_c[0] % 3
                            for ucl in range(4):
                                MM(banks[bk][:, :], yT[yk][:, ucl, til * 128:(til + 1) * 128], wob_sb[slo][:, ucl, hf * 512:(hf + 1) * 512],
                                   ucl == 0, ucl == 3, [b_yT[yk], b_wob[slo]], [bb[bk]])
                            xs = xres[:, ti, hf * 512:(hf + 1) * 512]
                            if ub == 0:
                                STT(xs, xs, ALPHA, banks[bk][:, :], ALU.mult, ALU.add, [b_xres[ti], bb[bk]], [b_xres[ti]])
                            else:
                                TT("dve", xs, xs, banks[bk][:, :], ALU.add, [b_xres[ti], bb[bk]], [b_xres[ti]])
                        if ub == 3:
                            ln_step(ti)
        flush_fin()
        P.fence()
        ffn(1, True)
        P.fence()
        P.emit(nc, st)
    return nc


def _host_x(inp, b):
    x = np.asarray(inp["x"], np.float32)[b]
    return {"x_tm": np.ascontiguousarray(x), "x_fm": np.ascontiguousarray(x.T)}


def _host_inputs(inp):
    f = np.float32
    d = {}
    d["wqkv"] = np.ascontiguousarray(np.asarray(inp["attn_w_qkv"], f)[0])
    d["wo"] = np.ascontiguousarray(np.asarray(inp["attn_w_o"], f)[0])
    d["win"] = np.ascontiguousarray(np.asarray(inp["gmlp_w_in"], f)[0])
    d["wout"] = np.ascontiguousarray(np.asarray(inp["gmlp_w_out"], f)[0])
    for l in range(2):
        for nm, key in (("wg", "ffn_w_gate"), ("wu", "ffn_w_up")):
            w = np.asarray(inp[key], f)[l]
            d["%s%d" % (nm, l)] = np.ascontiguousarray(w.reshape(8, 128, 22, 128).transpose(2, 1, 0, 3).reshape(22, 128, D))
        d["wd%d" % l] = np.ascontiguousarray(np.asarray(inp["ffn_w_down"], f)[l])
    lamv = np.concatenate([np.asarray(inp["attn_lambda_q1"], f)[0], np.asarray(inp["attn_lambda_q2"], f)[0],
                           np.asarray(inp["attn_lambda_k1"], f)[0], np.asarray(inp["attn_lambda_k2"], f)[0]])
    d["lamv"] = np.ascontiguousarray(np.broadcast_to(lamv[None, :], (128, 256)))
    d["gsub"] = np.ascontiguousarray(np.broadcast_to(np.asarray(inp["attn_subln_g"], f)[0][None, :], (128, 128)))
    b_in = np.asarray(inp["gmlp_b_in"], f)[0]
    d["binv"] = np.ascontiguousarray(np.broadcast_to(b_in[None, GH:], (128, GH)))
    cols = np.concatenate([b_in[:GH].reshape(16, 128).T, np.asarray(inp["gmlp_ln_g"], f)[0].reshape(16, 128).T,
                           np.asarray(inp["gmlp_ln_b"], f)[0].reshape(16, 128).T], axis=1)
    d["cols"] = np.ascontiguousarray(cols)
    d["wsT"] = np.ascontiguousarray(np.asarray(inp["gmlp_w_s"], f)[0].transpose(2, 0, 1))
    d["bsb"] = np.ascontiguousarray(np.broadcast_to(np.asarray(inp["gmlp_b_s"], f)[0][None], (128, 8, 128)))
    lnp = []
    for l in range(2):
        lnp += [np.asarray(inp["ln_mix_g"], f)[l], np.asarray(inp["ln_mix_b"], f)[l],
                np.asarray(inp["ln_ffn_g"], f)[l], np.asarray(inp["ln_ffn_b"], f)[l]]
    d["lnp"] = np.ascontiguousarray(np.broadcast_to(np.stack(lnp)[:, None, :], (8, 128, D)))
    return d


def _const_inputs():
    d = {}
    d["ident"] = np.eye(128, dtype=np.float32).astype(ml_dtypes.bfloat16)
    kl = np.arange(128, dtype=np.float32)[:, None, None]
    j = np.arange(4, dtype=np.float32)[None, :, None]
    ql = np.arange(128, dtype=np.float32)[None, None, :]
    bt = 128.0 * j - ql + kl
    btall = np.zeros((128, 5, 4, 128), np.float32)
    btall[:, 0] = bt
    kl2 = kl[:, 0, :]
    ql2 = ql[0]
    for jd in range(4):
        v = bt.copy()
        dg = 128.0 * jd - np.abs(ql2 - kl2)
        masked = (kl2 >= 64) & (ql2 < 64)
        dg = np.where(masked, NEG_BIG, dg)
        v[:, jd, :] = dg
        btall[:, 1 + jd] = v
    d["btall"] = btall.reshape(128, 5, 512)
    s = np.arange(128)[:, None]
    t = np.arange(128)[None, :]
    d["trilT"] = (t >= s).astype(np.float32)
    return d


_NC_CACHE = {}


def kernel(**inputs):
    x = np.asarray(inputs["x"])
    B, S, _ = x.shape
    if S not in _NC_CACHE:
        _NC_CACHE[S] = build(S)
    nc = _NC_CACHE[S]
    shared = _host_inputs(inputs)
    shared.update(_const_inputs())
    in_maps = []
    for b in range(B):
        d = dict(shared)
        d.update(_host_x(inputs, b))
        in_maps.append(d)
    res = run_bass_kernel_spmd(nc, in_maps, core_ids=list(range(B)))
    out = np.stack([np.asarray(r["out"]) for r in res.results], axis=0)
    return out.astype(np.float32)
```

```python
import math
import numpy as np
import ml_dtypes
import concourse.bass as bass
import concourse.mybir as mybir
from contextlib import ExitStack
from concourse.bass_utils import run_bass_kernel_spmd

F32 = mybir.dt.float32
BF16 = mybir.dt.bfloat16
AF = mybir.ActivationFunctionType
ALU = mybir.AluOpType

ENGS = ("pe", "act", "dve", "pool", "sp")
NDSEM = 24

D = 1024
H = 8
DFF = 2816
GH = 2048
ALPHA = 4.0 ** 0.25
EPS = 1e-5
LAMBDA_INIT0 = 0.8 - 0.6 * math.exp(0.0)
SLOPES = [2.0 ** (-(h + 1)) for h in range(H)]
NEG_BIG = -1.0e5


class Buf:
    __slots__ = ("w", "r")

    def __init__(self):
        self.w = None
        self.r = {}


class Prog:
    def __init__(self):
        self.ins = []
        self.streams = {e: [] for e in ENGS}
        self.ndma = 0
        self.dma_since_fence = []

    def op(self, eng, fn, reads=(), writes=(), dma=False, extra=None):
        i = len(self.ins)
        deps = {}
        for b in reads:
            if b.w is not None:
                deps[b.w] = "RAW"
        for b in writes:
            if b.w is not None:
                deps.setdefault(b.w, "WAW")
            for rid in b.r.values():
                deps.setdefault(rid, "WAR")
        if extra:
            for d in extra:
                deps[d] = "RAW"
        deps.pop(i, None)
        for b in reads:
            if dma:
                b.r[("dma", i)] = i
            else:
                b.r[eng] = i
        for b in writes:
            b.w = i
            b.r = {}
        rec = dict(eng=eng, fn=fn, deps=deps, dma=dma, sig=False, cnt=None, dsem=None, dval=None)
        if dma:
            rec["dsem"] = self.ndma % NDSEM
            rec["dval"] = 16 * (self.ndma // NDSEM + 1)
            self.ndma += 1
            self.dma_since_fence.append(i)
        self.ins.append(rec)
        self.streams[eng].append(i)
        return i

    def fence(self):
        last = []
        for e in ENGS:
            for i in reversed(self.streams[e]):
                if self.ins[i]["fn"] is not None and not self.ins[i]["dma"]:
                    last.append(i)
                    break
        deps = last + self.dma_since_fence
        self.dma_since_fence = []
        for e in ENGS:
            self.op(e, None, extra=deps)

    def emit(self, nc, stack):
        ins = self.ins
        for i, rec in enumerate(ins):
            keep = {}
            for d, kind in rec["deps"].items():
                p = ins[d]
                if p["dma"]:
                    keep[d] = kind
                    continue
                if p["eng"] == rec["eng"] and not rec["dma"]:
                    if rec["eng"] == "pe" or kind != "RAW":
                        continue
                keep[d] = kind
                p["sig"] = True
            rec["deps"] = keep
        cnt = {e: 0 for e in ENGS}
        for e in ENGS:
            for i in self.streams[e]:
                rec = ins[i]
                if rec["sig"] and not rec["dma"]:
                    cnt[e] += 1
                    rec["cnt"] = cnt[e]
        esem = {e: stack.enter_context(nc.semaphore("s_" + e)) for e in ENGS}
        dsem = [stack.enter_context(nc.semaphore("d_%d" % k)) for k in range(NDSEM)]
        block = stack.enter_context(nc.Block())

        def run_stream(ename, eobj):
            seen = {}
            for i in self.streams[ename]:
                rec = ins[i]
                waits = {}
                for d in rec["deps"]:
                    p = ins[d]
                    if p["dma"]:
                        key = ("d", p["dsem"])
                        val = p["dval"]
                    else:
                        key = ("e", p["eng"])
                        val = p["cnt"]
                    if waits.get(key, 0) < val:
                        waits[key] = val
                if rec["dma"] and rec["dval"] > 16:
                    key = ("d", rec["dsem"])
                    if waits.get(key, 0) < rec["dval"] - 16:
                        waits[key] = rec["dval"] - 16
                for key, val in waits.items():
                    if seen.get(key, 0) >= val:
                        continue
                    seen[key] = val
                    sem = dsem[key[1]] if key[0] == "d" else esem[key[1]]
                    eobj.wait_ge(sem, val)
                if rec["fn"] is None:
                    continue
                r = rec["fn"](eobj)
                if rec["dma"]:
                    r.then_inc(dsem[rec["dsem"]], 16)
                elif rec["sig"]:
                    r.then_inc(esem[ename], 1)

        @block.tensor
        def _(e):
            run_stream("pe", e)

        @block.scalar
        def _(e):
            run_stream("act", e)

        @block.vector
        def _(e):
            run_stream("dve", e)

        @block.gpsimd
        def _(e):
            run_stream("pool", e)

        @block.sync
        def _(e):
            run_stream("sp", e)


def build(S, stop_after=None):
    nc = bass.Bass("TRN2", target_bir_lowering=False)
    NT = S // 128
    P = Prog()

    def din(name, shape, dt=F32):
        return nc.dram_tensor(name, list(shape), dt, kind="ExternalInput").ap()

    x_tm = din("x_tm", [S, D])
    x_fm = din("x_fm", [D, S])
    wqkv = din("wqkv", [D, 3 * D])
    wo_d = din("wo", [D, D])
    win_d = din("win", [D, 2 * GH])
    wout_d = din("wout", [GH, D])
    wg_d = [din("wg%d" % l, [22, 128, D]) for l in range(2)]
    wu_d = [din("wu%d" % l, [22, 128, D]) for l in range(2)]
    wd_d = [din("wd%d" % l, [DFF, D]) for l in range(2)]
    lam_d = din("lamv", [128, 256])
    gsub_d = din("gsub", [128, 128])
    binv_d = din("binv", [128, GH])
    cols_d = din("cols", [128, 48])
    wsT_d = din("wsT", [128, 8, 128])
    bsb_d = din("bsb", [128, 8, 128])
    lnp_d = din("lnp", [8, 128, D])
    ident_d = din("ident", [128, 128], BF16)
    btall_d = din("btall", [128, 5, 512])
    trilT_d = din("trilT", [128, 128])
    out_d = nc.dram_tensor("out", [S, D], F32, kind="ExternalOutput").ap()

    with ExitStack() as st:
        ARENA_B = 208896
        arena = st.enter_context(nc.sbuf_tensor("arena", [128, ARENA_B // 2], BF16))

        def view(off, shape, dt):
            n = 1
            for s_ in shape:
                n *= s_
            nb = n * (4 if dt == F32 else 2)
            assert off % 4 == 0 and off + nb <= ARENA_B, (off, nb)
            ap = arena[:, off // 2:(off + nb) // 2]
            if dt == F32:
                ap = ap.bitcast(F32)
            if len(shape) == 2:
                ap = ap.rearrange("p (a b) -> p a b", a=shape[0])
            elif len(shape) == 3:
                ap = ap.rearrange("p (a b c) -> p a b c", a=shape[0], b=shape[1])
            return ap

        PSA = st.enter_context(nc.psum_tensor("psa", [128, 3584], F32))
        banks = [PSA[:, i * 512:(i + 1) * 512] for i in range(7)]
        bankT = st.enter_context(nc.psum_tensor("pbT", [128, 1024], BF16))
        bb = [Buf() for _ in range(7)]
        bbT = Buf()

        xres = view(0, [NT, D], F32)
        xT = view(65536, [8, S], BF16)
        gam = view(98304, [D], F32)
        bet = view(102400, [D], F32)
        xb = view(106496, [2, D], BF16)
        ident = view(110592, [128], BF16)
        ones_bf = view(110848, [128], BF16)
        smalls = view(111104, [256], F32)
        cols = view(112128, [48], F32)
        lamt = view(112128 + 192, [256], F32)
        lamtmp = view(112128 + 192 + 1024, [128], F32)
        PH = 114176
        b_xres = [Buf() for _ in range(NT)]
        b_xT = [Buf() for _ in range(NT)]
        b_gb = Buf()
        b_xb = [Buf(), Buf()]
        b_ident = Buf()
        b_ones = Buf()
        b_cols = Buf()
        b_lam = Buf()
        b_sm = {}

        def smb(key):
            if key not in b_sm:
                b_sm[key] = Buf()
            return b_sm[key]

        def MM(out, lhsT, rhs, start, stop, R, W, sgc=False):
            P.op("pe", lambda e: e.matmul(out, lhsT=lhsT, rhs=rhs, start=start, stop=stop, skip_group_check=sgc), reads=R, writes=W)

        def TR(out, in_, R, W):
            P.op("pe", lambda e: e.transpose(out=out, in_=in_, identity=ident), reads=list(R) + [b_ident], writes=W)

        def ACTV(out, in_, func, R, W, **kw):
            P.op("act", lambda e: e.activation(out=out, in_=in_, func=func, **kw), reads=R, writes=W)

        def STT(out, in0, scalar, in1, op0, op1, R, W):
            P.op("dve", lambda e: e.scalar_tensor_tensor(out=out, in0=in0, scalar=scalar, in1=in1, op0=op0, op1=op1), reads=R, writes=W)

        def TT(eng, out, in0, in1, op, R, W):
            P.op(eng, lambda e: e.tensor_tensor(out=out, in0=in0, in1=in1, op=op), reads=R, writes=W)

        def TS(eng, out, in0, s1, s2, op0, op1, R, W):
            if op1 is None:
                P.op(eng, lambda e: e.tensor_scalar(out=out, in0=in0, scalar1=s1, scalar2=None, op0=op0), reads=R, writes=W)
            else:
                P.op(eng, lambda e: e.tensor_scalar(out=out, in0=in0, scalar1=s1, scalar2=s2, op0=op0, op1=op1), reads=R, writes=W)

        def CP(eng, out, in_, R, W):
            if eng == "act":
                ACTV(out, in_, AF.Copy, R, W)
            else:
                P.op(eng, lambda e: e.tensor_copy(out=out, in_=in_), reads=R, writes=W)

        def DMA(q, out, in_, R, W):
            return P.op(q, lambda e: e.dma_start(out=out, in_=in_), reads=R, writes=W, dma=True)

        xv = x_tm.rearrange("(t p) d -> p t d", p=128)
        xfv = x_fm.rearrange("(kc p) s -> p kc s", p=128)
        def load_xT(tt):
            DMA("pool", xT[:, :, tt * 512:(tt + 1) * 512], xfv[:, :, tt * 512:(tt + 1) * 512], [], b_xT[tt * 4:tt * 4 + 4])

        load_xT(0)

        def load_xres():
            for t0 in range(0, NT, 4):
                DMA("sp", xres[:, t0:t0 + 4, :], xv[:, t0:t0 + 4, :], [], b_xres[t0:t0 + 4])
        DMA("sp", ident, ident_d, [], [b_ident])
        DMA("sp", cols, cols_d, [], [b_cols])
        DMA("sp", lamt, lam_d, [], [b_lam])
        P.op("pool", lambda e: e.memset(ones_bf, 1.0), writes=[b_ones])
        negh = smalls[:, 0:1]
        P.op("pool", lambda e: e.memset(negh, -0.5), writes=[smb("negh")])
        epsc = smalls[:, 7:8]
        P.op("pool", lambda e: e.memset(epsc, EPS), writes=[smb("epsc")])

        TT("dve", lamtmp, lamt[:, 0:128], lamt[:, 128:256], ALU.mult, [b_lam], [smb("lamtmp")])
        s12 = smalls[:, 1:3]
        P.op("dve", lambda e: e.tensor_reduce(out=s12, in_=lamtmp.rearrange("p (a b) -> p a b", a=2), axis=mybir.AxisListType.X, op=ALU.add),
             reads=[smb("lamtmp")], writes=[smb("s12")])
        e12 = smalls[:, 3:5]
        ACTV(e12, s12, AF.Exp, [smb("s12")], [smb("e12")])
        lamd = smalls[:, 5:6]
        TT("dve", lamd, e12[:, 0:1], e12[:, 1:2], ALU.subtract, [smb("e12")], [smb("lamd")])
        neglam = smalls[:, 6:7]
        TS("dve", neglam, lamd, LAMBDA_INIT0, -1.0, ALU.add, ALU.mult, [smb("lamd")], [smb("neglam")])

        SM_LN = 8
        SM_AT = 48
        SM_GV = 64

        def load_ln(idx_g, idx_b):
            DMA("sp", gam, lnp_d[idx_g], [], [b_gb])
            DMA("sp", bet, lnp_d[idx_b], [], [b_gb])

        ln_ctr = [0]

        def ln_tail(ti, last=False):
            k = ln_ctr[0] % 2
            ln_ctr[0] += 1
            base = SM_LN + 20 * k
            stt = smalls[:, base:base + 12]
            mv = smalls[:, base + 12:base + 14]
            ve = smalls[:, base + 14:base + 15]
            rstd = smalls[:, base + 15:base + 16]
            nmr = smalls[:, base + 16:base + 17]
            bs = smb(("ln", k))
            r = xres[:, ti, :]
            for hf in range(2):
                P.op("dve", lambda e, hf=hf: e.bn_stats(out=stt[:, hf * 6:hf * 6 + 6], in_=xres[:, ti, hf * 512:(hf + 1) * 512]),
                     reads=[b_xres[ti]], writes=[bs])
            P.op("dve", lambda e: e.bn_aggr(out=mv, in_=stt), reads=[bs], writes=[bs])
            TS("dve", ve, mv[:, 1:2], EPS, None, ALU.add, None, [bs], [bs])
            TT("pool", rstd, ve, negh, ALU.pow, [bs, smb("negh")], [bs])
            def stage_a2():
                STT(nmr, mv[:, 0:1], -1.0, rstd, ALU.mult, ALU.mult, [bs], [bs])
                ACTV(r, r, AF.Identity, [bs, b_xres[ti]], [b_xres[ti]], scale=rstd, bias=nmr)
                TT("pool", r, r, gam, ALU.mult, [b_xres[ti], b_gb], [b_xres[ti]])

            def stage_b():
                TT("dve", r, r, bet, ALU.add, [b_xres[ti], b_gb], [b_xres[ti]])
                if not last:
                    ACTV(xb[:, k, :], r, AF.Copy, [b_xres[ti]], [b_xb[k]])
                else:
                    DMA("sp", out_d[ti * 128:(ti + 1) * 128, :], xres[:, ti, :], [b_xres[ti]], [])

            def stage_c():
                if last:
                    return
                for kc in range(8):
                    TR(bankT[:, kc * 128:(kc + 1) * 128], xb[:, k, kc * 128:(kc + 1) * 128], [b_xb[k]], [bbT])
                CP("act", xT[:, :, ti * 128:(ti + 1) * 128], bankT.ap().rearrange("p (a b) -> p a b", a=8), [bbT], [b_xT[ti]])
            return [stage_a2, stage_b, stage_c]

        pend_fin = []

        def ln_step(ti, last=False):
            l1 = [e_ for e_ in pend_fin if len(e_) == 1]
            l2 = [e_ for e_ in pend_fin if len(e_) == 2]
            l3 = [e_ for e_ in pend_fin if len(e_) == 3]
            for e_ in l1:
                e_.pop(0)()
            ent_new = ln_tail(ti, last=last)
            for e_ in l3:
                e_.pop(0)()
            for e_ in l2:
                e_.pop(0)()
            for e_ in list(pend_fin):
                if not e_:
                    pend_fin.remove(e_)
            pend_fin.append(ent_new)

        def flush_fin(keep=0):
            while pend_fin:
                for ent in list(pend_fin):
                    ent.pop(0)()
                    if not ent:
                        pend_fin.remove(ent)
                if keep:
                    break

        o = PH
        qT = [view(o + i * 2 * S, [S], BF16) for i in range(2)]; o += 4 * S
        kT = [view(o + i * 2 * S, [S], BF16) for i in range(2)]; o += 4 * S
        VA_B = ((NT * 129 * 2 + 3) // 4) * 4
        Vaug = [view(o + i * VA_B, [NT, 129], BF16) for i in range(2)]; o += 2 * VA_B
        wqk1 = [view(o + j * 2048, [8, 128], BF16) for j in range(3)]; o += 3 * 2048
        wqk = [wqk1, wqk1]
        Sf = [view(o + i * 4096, [2, 512], F32) for i in range(2)]; o += 8192
        oT = view(o, [8, S], BF16); o_oT = o; o += 16 * S
        Pt = [view(o + i * 2048, [2, 512], BF16) for i in range(3)]; o += 6144
        btall = view(o, [5, 512], F32); o += 10240
        gst = view(o, [128], F32); o += 512
        t1t = [view(o + i * 512, [128], F32) for i in range(2)]; o += 1024
        at = [view(o + i * 512, [128], F32) for i in range(2)]; o += 1024
        obt = [view(o + i * 256, [128], BF16) for i in range(2)]; o += 512
        junk = view(o, [128], F32); o += 512
        Osb = [view(o + i * 1040, [2, 130], F32) for i in range(2)]; o += 2080
        b_Osb = [Buf(), Buf()]
        assert o <= ARENA_B, o
        wo_sb = view(PH, [8, D], BF16)
        assert PH + 16384 <= o_oT

        b_q = [Buf(), Buf()]; b_k = [Buf(), Buf()]; b_v = [Buf(), Buf()]
        b_w1 = [Buf() for _ in range(3)]
        b_w = [b_w1, b_w1]
        b_Sf = [Buf(), Buf()]
        b_oT = [Buf() for _ in range(NT)]
        b_Pt = [Buf() for _ in range(3)]
        b_bt = Buf(); b_gs = Buf()
        b_t1 = [Buf(), Buf()]; b_a = [Buf(), Buf()]; b_ob = [Buf(), Buf()]; b_junk = Buf()

        DMA("sp", btall, btall_d, [], [b_bt])
        DMA("sp", gst, gsub_d, [], [b_gs])
        TS("dve", gst, gst, 1.0 - LAMBDA_INIT0, None, ALU.mult, None, [b_gs], [b_gs])
        for i in range(2):
            P.op("pool", lambda e, i=i: e.memset(Vaug[i][:, :, 128:129], 1.0), writes=[b_v[i]])
        load_ln(0, 1)

        wv3 = wqkv.rearrange("(kc p) n -> p kc n", p=128)
        misc_rot = [0]

        def misc_bank():
            return 6

        def proj_items(h):
            hb = h % 2
            items = []

            def ld():
                for j in range(3):
                    c0 = j * D + h * 128
                    DMA("pool", wqk[hb][j], wv3[:, :, c0:c0 + 128], [], [b_w[hb][j]])
            items.append(ld)
            for j, (dst, bd) in enumerate(((qT, b_q), (kT, b_k))):
                for tt in range(S // 512):
                    def f(j=j, dst=dst, bd=bd, tt=tt):
                        bk = misc_bank()
                        for kc in range(8):
                            MM(banks[bk][:, :], wqk[hb][j][:, kc, :], xT[:, kc, tt * 512:(tt + 1) * 512], kc == 0, kc == 7,
                               [b_w[hb][j]] + b_xT[tt * 4:tt * 4 + 4], [bb[bk]])
                        CP("act", dst[hb][:, tt * 512:(tt + 1) * 512], banks[bk][:, :], [bb[bk]], [bd[hb]])
                    items.append(f)
            for vb in range(NT // 4):
                def f(vb=vb):
                    bk = misc_bank()
                    for tl in range(4):
                        ti = vb * 4 + tl
                        for kc in range(8):
                            MM(banks[bk][:, tl * 128:(tl + 1) * 128], xT[:, kc, ti * 128:(ti + 1) * 128], wqk[hb][2][:, kc, :], kc == 0, kc == 7,
                               [b_w[hb][2], b_xT[ti]], [bb[bk]])
                    CP("dve", Vaug[hb][:, vb * 4:vb * 4 + 4, 0:128], banks[bk].rearrange("p (a b) -> p a b", a=4), [bb[bk]], [b_v[hb]])
                items.append(f)
            return items

        s_rot = [0]
        o_rot = [0]
        p_rot = [0]

        Sx = [PSA[:, 0:1024], PSA[:, 1024:2048]]
        bSx = [[bb[0], bb[1]], [bb[2], bb[3]]]

        def attention_head(h, nxt):
            hb = h % 2
            slope = SLOPES[h]
            groups = []
            for qi in range(NT):
                o_rot[0] += 1
                ob = 4 + (o_rot[0] % 2)
                for g in range(qi // 4 + 1):
                    s_rot[0] += 1
                    p_rot[0] += 1
                    groups.append(dict(qi=qi, g=g, ob=ob, sk=s_rot[0] % 2, pk=p_rot[0] % 3, fk=s_rot[0] % 2,
                                       tiles=list(range(4 * g, min(4 * g + 4, qi + 1))), lastq=(g == qi // 4)))

            def scores(G):
                qi, g, sk, pk, tiles = G["qi"], G["g"], G["sk"], G["pk"], G["tiles"]
                n = len(tiles) * 128
                for j, kt in enumerate(tiles):
                    for c in range(2):
                        MM(Sx[sk][:, c * 512 + j * 128:c * 512 + (j + 1) * 128], kT[hb][64 * c:64 * c + 64, kt * 128:(kt + 1) * 128],
                           qT[hb][64 * c:64 * c + 64, qi * 128:(qi + 1) * 128], True, True, [b_k[hb], b_q[hb]], bSx[sk])
                var = (1 + qi - 4 * g) if (4 * g + 3 >= qi) else 0
                sv = Sx[sk].rearrange("p (c n) -> p c n", c=2)[:, :, 0:n]
                fk = G["fk"]
                STT(Sf[fk][:, :, 0:n], btall[:, var, 0:n].unsqueeze(1).broadcast_to([128, 2, n]), 8.0 * slope, sv, ALU.mult, ALU.add,
                    [b_bt] + bSx[sk], [b_Sf[fk]])
                ACTV(Pt[pk][:, :, 0:n], Sf[fk][:, :, 0:n], AF.Exp, [b_Sf[fk]], [b_Pt[pk]], scale=0.125, bias=float(-slope * 128.0 * (qi - 4 * g)))

            def av(G):
                qi, pk, tiles, ob = G["qi"], G["pk"], G["tiles"], G["ob"]
                for j, kt in enumerate(tiles):
                    for c in range(2):
                        MM(banks[ob][:, c * 256:c * 256 + 129], Pt[pk][:, c, j * 128:(j + 1) * 128], Vaug[hb][:, kt, :],
                           (kt == 0 and c == 0), kt == qi, [b_Pt[pk], b_v[hb]], [bb[ob]], sgc=True)
                if G["lastq"]:
                    pipe.append(post_stages(qi, ob))

            def post_stages(qi, ob):
                k2 = qi % 2
                base = SM_AT + 8 * k2
                r12 = smalls[:, base:base + 2]; r2l = smalls[:, base + 2:base + 3]
                ss = smalls[:, base + 3:base + 4]; ms = smalls[:, base + 4:base + 5]; rs = smalls[:, base + 5:base + 6]
                bs = smb(("at", k2))
                bs2 = smb(("at2", k2))
                ov = banks[ob].rearrange("p (c n) -> p c n", c=2)
                osb = Osb[k2]

                def s1():
                    ACTV(osb[:, :, 0:129], ov[:, :, 0:129], AF.Copy, [bb[ob]], [b_Osb[k2]])

                def s2():
                    P.op("dve", lambda e: e.reciprocal(out=r12.unsqueeze(2), in_=osb[:, :, 128:129]), reads=[b_Osb[k2]], writes=[bs])
                    TT("dve", r2l, r12[:, 1:2], neglam, ALU.mult, [bs, smb("neglam")], [bs])
                    TS("dve", t1t[k2], osb[:, 0, 0:128], r12[:, 0:1], None, ALU.mult, None, [b_Osb[k2], bs], [b_t1[k2]])
                    STT(at[k2], osb[:, 1, 0:128], r2l, t1t[k2], ALU.mult, ALU.add, [b_Osb[k2], bs, b_t1[k2]], [b_a[k2]])

                def s3():
                    ACTV(junk, at[k2], AF.Square, [b_a[k2]], [b_junk, bs2], accum_out=ss)
                    ACTV(ms, ss, AF.Ln, [bs2, smb("epsc")], [bs2], scale=1.0 / 128.0, bias=epsc)
                    ACTV(rs, ms, AF.Exp, [bs2], [bs2], scale=-0.5)

                def s4():
                    STT(obt[k2], at[k2], rs, gst, ALU.mult, ALU.mult, [b_a[k2], bs2, b_gs], [b_ob[k2]])

                def s5():
                    TR(bankT[:, 0:128], obt[k2], [b_ob[k2]], [bbT])
                    CP("act", oT[:, h, qi * 128:(qi + 1) * 128], bankT[:, 0:128], [bbT], [b_oT[qi]])
                return [s1, s2, s3, s4, s5]

            def pump_posts():
                for ent in list(pipe):
                    ent.pop(0)()
                    if not ent:
                        pipe.remove(ent)

            if h == H - 1 and S == 2048:
                for hh in range(H):
                    DMA("pool", wo_c[hh], wov[:, hh, :], [], [wo_cb[hh]])
            every = max(1, len(groups) // (len(nxt) + 1)) if nxt else 0
            scores(groups[0])
            if len(groups) > 1:
                scores(groups[1])
            for n, G in enumerate(groups):
                if n + 2 < len(groups):
                    scores(groups[n + 2])
                pump_posts()
                av(G)
                if nxt and (n % every == every - 1):
                    nxt.pop(0)()
            if h == H - 1:
                while pipe:
                    pump_posts()
            while nxt:
                nxt.pop(0)()

        pipe = []
        wov = wo_d.rearrange("(kc p) n -> p kc n", p=128)
        b_wo = Buf()
        if S == 2048:
            wo_off = [PH + 0, PH + 2048, PH + 8192, PH + 10240, PH + 16384, PH + 18432, PH + 16384 + 2 * VA_B, PH + 16384 + 2 * VA_B + 2048]
            wo_c = [view(off_, [D], BF16) for off_ in wo_off]
            wo_cb = [b_q[0], b_q[0], b_k[0], b_k[0], b_v[0], b_v[0], b_w1[0], b_w1[1]]
        else:
            wo_c = [wo_sb[:, hh, :] for hh in range(H)]
            wo_cb = [b_wo] * H
        items0 = proj_items(0)
        items0.pop(0)()
        for tt in range(1, S // 512):
            load_xT(tt)
        nq_ = S // 512
        q_, k_, v_ = items0[0:nq_], items0[nq_:2 * nq_], items0[2 * nq_:]
        for tt in range(nq_):
            q_[tt]()
            k_[tt]()
            v_[tt]()
        load_xres()
        for h in range(H):
            attention_head(h, proj_items(h + 1) if h + 1 < H else [])

        P.fence()
        if S != 2048:
            for kc0 in range(0, 8, 2):
                DMA("pool", wo_sb[:, kc0:kc0 + 2, :], wov[:, kc0:kc0 + 2, :], [], [b_wo])
        rot7 = [0]

        def nb7():
            rot7[0] += 1
            return rot7[0] % 7

        for ti in range(NT):
            for hf in range(2):
                bk = nb7()
                for h in range(H):
                    MM(banks[bk][:, :], oT[:, h, ti * 128:(ti + 1) * 128], wo_c[h][:, hf * 512:(hf + 1) * 512], h == 0, h == H - 1,
                       [b_oT[ti], wo_cb[h]], [bb[bk]])
                xs = xres[:, ti, hf * 512:(hf + 1) * 512]
                STT(xs, xs, ALPHA, banks[bk][:, :], ALU.mult, ALU.add, [b_xres[ti], bb[bk]], [b_xres[ti]])
            ln_step(ti)
        flush_fin()
        P.fence()

        NTT = S // 512
        QP = [(0, 6), (6, 6), (12, 5), (17, 5)]
        o = PH
        hT = view(o, [6, S], BF16); o += 12 * S
        wd_sb = [view(o + i * 2048, [D], BF16) for i in range(12)]; o += 12 * 2048
        wg_sb = [view(o + i * 2048, [D], BF16) for i in range(4)]; o += 8192
        wu_sb = [view(o + i * 2048, [D], BF16) for i in range(4)]; o += 8192
        sgt = [view(o + i * 2048, [512], F32) for i in range(2)]; o += 4096
        assert o <= ARENA_B, o
        FFN_END = o

        def ffn(l, last_layer, hook=None):
            b_hT = [[Buf() for _ in range(NTT)] for _ in range(6)]
            b_wd = [Buf() for _ in range(12)]
            b_wg = [Buf() for _ in range(4)]
            b_wu = [Buf() for _ in range(4)]
            b_sg = [Buf(), Buf()]
            load_ln(2 + 4 * l, 3 + 4 * l)
            wgv = wg_d[l].rearrange("f p n -> p f n")
            wuv = wu_d[l].rearrange("f p n -> p f n")
            wdv = wd_d[l].rearrange("(fc p) n -> p fc n", p=128)
            issued = [0]

            def issue_load():
                fc = issued[0]
                if fc >= 22:
                    return
                issued[0] += 1
                DMA("pool", wg_sb[fc % 4], wgv[:, fc, :], [], [b_wg[fc % 4]])
                DMA("pool", wu_sb[fc % 4], wuv[:, fc, :], [], [b_wu[fc % 4]])
                DMA("pool", wd_sb[fc % 12], wdv[:, fc, :], [], [b_wd[fc % 12]])

            gu_rot = [0]
            dn_rot = [0]
            sg_rot = [0]
            for qp, (c0, cn) in enumerate(QP):
                for j in range(cn):
                    fc = c0 + j
                    while issued[0] <= min(fc + 3, 21):
                        issue_load()
                    sl = fc % 4
                    for tt in range(NTT):
                        gu_rot[0] += 1
                        bg = (gu_rot[0] % 2) * 2
                        bu = bg + 1
                        xb_ = b_xT[tt * 4:tt * 4 + 4]
                        rhs_tok = slice(tt * 512, (tt + 1) * 512)
                        for kc in range(8):
                            MM(banks[bg][:, :], wg_sb[sl][:, kc * 128:(kc + 1) * 128], xT[:, kc, rhs_tok], kc == 0, kc == 7,
                               [b_wg[sl]] + xb_, [bb[bg]])
                        for kc in range(8):
                            MM(banks[bu][:, :], wu_sb[sl][:, kc * 128:(kc + 1) * 128], xT[:, kc, rhs_tok], kc == 0, kc == 7,
                               [b_wu[sl]] + xb_, [bb[bu]])
                        sg_rot[0] += 1
                        sk = sg_rot[0] % 2
                        ACTV(sgt[sk], banks[bg][:, :], AF.Silu, [bb[bg]], [b_sg[sk]])
                        TT("dve", hT[:, j, tt * 512:(tt + 1) * 512], sgt[sk], banks[bu][:, :], ALU.mult, [b_sg[sk], bb[bu]], [b_hT[j][tt]])
                if hook is not None and qp == 0:
                    hook()
                lastp = qp == len(QP) - 1
                for ti in range(NT):
                    for hf in range(2):
                        dn_rot[0] += 1
                        bk = 4 + (dn_rot[0] % 3)
                        for j in range(cn):
                            fc = c0 + j
                            MM(banks[bk][:, :], hT[:, j, ti * 128:(ti + 1) * 128], wd_sb[fc % 12][:, hf * 512:(hf + 1) * 512], j == 0, j == cn - 1,
                               [b_hT[j][ti // 4], b_wd[fc % 12]], [bb[bk]])
                        xs = xres[:, ti, hf * 512:(hf + 1) * 512]
                        if qp == 0:
                            STT(xs, xs, ALPHA, banks[bk][:, :], ALU.mult, ALU.add, [b_xres[ti], bb[bk]], [b_xres[ti]])
                        else:
                            TT("dve", xs, xs, banks[bk][:, :], ALU.add, [b_xres[ti], bb[bk]], [b_xres[ti]])
                    if lastp:
                        ln_step(ti, last=last_layer)
                flush_fin()

        TG2 = min(512, S)
        NB = TG2 // 128
        SGT = min(1024, S)
        NSG = S // SGT
        GPS = SGT // TG2
        NVB = GPS * NB
        o = PH
        vraw = view(o, [NVB, GH], BF16); o_vraw = o; o += NVB * GH * 2
        yT = [view(o + i * 4 * TG2 * 2, [4, TG2], BF16) for i in range(2)]; o_yT = o; o += 2 * 4 * TG2 * 2
        win_sb = [view(o + i * 8192, [8, 512], BF16) for i in range(2)]; o += 2 * 8192
        wob_sb = [view(o + i * 8192, [4, D], BF16) for i in range(2)]; o_wob = o; o += 2 * 8192
        binv = [view(o + i * 2048, [512], F32) for i in range(2)]; o_ug = o; o += 4096
        ugt = binv
        bias2 = view(o, [16, 128], F32); o += 8192
        WmT = view(o, [8, 128], BF16); o += 2048
        tmpv = view(o, [512], F32); o_tmpv = o; o += 2048
        t2t = tmpv
        gsm = view(o, [256], F32); o += 1024
        assert o <= ARENA_B, o
        bsb = view(o_ug, [8, 128], F32)
        wsT = view(o_wob + 8192 + 4096, [8, 128], F32)
        trilT = view(o_tmpv, [128], F32)
        b_vraw = [Buf() for _ in range(NVB)]
        b_yT = [Buf(), Buf()]
        b_win = [Buf(), Buf()]
        b_wob = [Buf(), Buf()]
        b_binv = [Buf(), Buf()]; b_bias2 = Buf(); b_WmT = Buf()
        b_tmpv = Buf(); b_ug = b_binv; b_t2 = b_tmpv
        b_prep = Buf()
        assert min(o_wob + 8192 + 4096, o_ug, o_tmpv) >= FFN_END, (o_wob, o_ug, o_tmpv, FFN_END)
        DMA("sp", bsb, bsb_d, [], [b_prep])
        DMA("sp", wsT, wsT_d, [], [b_prep])
        DMA("sp", trilT, trilT_d, [], [b_prep])
        def gmlp_prep_compute():
            TT("dve", WmT, wsT, trilT.unsqueeze(1).broadcast_to([128, 8, 128]), ALU.mult, [b_prep], [b_WmT])
            for g in range(8):
                bk = g // 4
                MM(banks[bk][:, (g % 4) * 128:(g % 4 + 1) * 128], ones_bf, WmT[:, g, :], True, True, [b_ones, b_WmT], [bb[bk]])
            for uc in range(16):
                g = uc // 2
                STT(bias2[:, uc, :], banks[g // 4][:, (g % 4) * 128:(g % 4 + 1) * 128], cols[:, 32 + uc:33 + uc], bsb[:, g, :], ALU.mult, ALU.add,
                    [bb[g // 4], b_cols, b_prep], [b_bias2])

        ffn(0, False, hook=gmlp_prep_compute)
        P.fence()


        winv = win_d.rearrange("(kc p) n -> p kc n", p=128)
        woutv = wout_d.rearrange("(c p) n -> p c n", p=128)
        wl = []
        for sg in range(NSG):
            for vb in range(4):
                wl.append(("win", GH + vb * 512))
            for ub in range(4):
                wl.append(("win", ub * 512))
                wl.append(("wob", ub * 4))
        wstate = {"next": 0, "win": 0, "wob": 0}
        wslot = {}

        def issue_w():
            i = wstate["next"]
            if i >= len(wl):
                return
            wstate["next"] += 1
            kind, a_ = wl[i]
            if kind == "win":
                sl = wstate["win"] % 2
                wstate["win"] += 1
                wslot[i] = sl
                DMA("pool", win_sb[sl], winv[:, :, a_:a_ + 512], [], [b_win[sl]])
            else:
                sl = wstate["wob"] % 2
                wstate["wob"] += 1
                wslot[i] = sl
                DMA("pool", wob_sb[sl], woutv[:, a_:a_ + 4, :], [], [b_wob[sl]])

        issue_w()
        issue_w()
        load_ln(4, 5)
        wi = [0]

        def next_w(ahead):
            i = wi[0]
            wi[0] += 1
            while wstate["next"] <= i + ahead and wstate["next"] < len(wl):
                issue_w()
            return wslot[i]

        rot_a = [0]
        rot_b = [0]
        rot_c = [0]
        for sg in range(NSG):
            tok_sg = sg * SGT
            tib0 = tok_sg // 128
            vst = gsm[:, 0:NVB * 24]
            b_vst = [smb(("vst", t)) for t in range(NVB)]
            for vb in range(4):
                sl = next_w(1)
                bvk = vb % 2
                DMA("sp", binv[bvk], binv_d[:, vb * 512:(vb + 1) * 512], [], [b_binv[bvk]])
                for til in range(NVB):
                    ti = tib0 + til
                    rot_a[0] += 1
                    bk = rot_a[0] % 3
                    for kc in range(8):
                        MM(banks[bk][:, :], xT[:, kc, ti * 128:(ti + 1) * 128], win_sb[sl][:, kc, :], kc == 0, kc == 7,
                           [b_xT[ti], b_win[sl]], [bb[bk]])
                    TT("dve", tmpv, banks[bk][:, :], binv[bvk], ALU.add, [bb[bk], b_binv[bvk]], [b_tmpv])
                    ACTV(vraw[:, til, vb * 512:(vb + 1) * 512], tmpv, AF.Gelu, [b_tmpv], [b_vraw[til]])
                    P.op("dve", lambda e, til=til, vb=vb: e.bn_stats(out=vst[:, til * 24 + vb * 6:til * 24 + vb * 6 + 6],
                                                                  in_=vraw[:, til, vb * 512:(vb + 1) * 512]),
                         reads=[b_vraw[til]], writes=[b_vst[til]])
                flush_fin(keep=1)
            for til in range(NVB):
                base = NVB * 24 + til * 4
                mv = gsm[:, base:base + 2]; ve = gsm[:, base + 2:base + 3]; rstd = gsm[:, base + 3:base + 4]
                bs = b_vst[til]
                P.op("dve", lambda e, til=til, mv=mv: e.bn_aggr(out=mv, in_=vst[:, til * 24:(til + 1) * 24]), reads=[bs], writes=[bs])
                TS("dve", ve, mv[:, 1:2], EPS, None, ALU.add, None, [bs], [bs])
                TT("pool", rstd, ve, negh, ALU.pow, [bs, smb("negh")], [bs])
            for til in range(NVB):
                base = NVB * 24 + til * 4
                mv = gsm[:, base:base + 2]; rstd = gsm[:, base + 3:base + 4]
                bs = b_vst[til]
                TS("dve", vraw[:, til, :], vraw[:, til, :], mv[:, 0:1], rstd, ALU.subtract, ALU.mult, [bs, b_vraw[til]], [b_vraw[til]])
            for ub in range(4):
                sl = next_w(1)
                slo = next_w(1)
                for gi in range(GPS):
                    tok0 = tok_sg + gi * TG2
                    tib = tok0 // 128
                    yk = (ub * GPS + gi) % 2
                    for ucl in range(4):
                        uc = ub * 4 + ucl
                        gg = uc // 2
                        rot_b[0] += 1
                        bu = 3 + (rot_b[0] % 2)
                        for kc in range(8):
                            MM(banks[bu][:, 0:TG2], win_sb[sl][:, kc, ucl * 128:(ucl + 1) * 128], xT[:, kc, tok0:tok0 + TG2], kc == 0, kc == 7,
                               [b_win[sl]] + b_xT[tib:tib + NB], [bb[bu]])
                        k = rot_b[0] % 2
                        ACTV(ugt[k][:, 0:TG2], banks[bu][:, 0:TG2], AF.Gelu, [bb[bu], b_cols], [b_ug[k]], bias=cols[:, uc:uc + 1])
                        bs_ = 5 + (rot_b[0] % 2)
                        for blk in range(NB):
                            vb_ = gi * NB + blk
                            MM(banks[bs_][:, blk * 128:(blk + 1) * 128], vraw[:, vb_, uc * 128:(uc + 1) * 128], WmT[:, gg, :], True, True,
                               [b_vraw[vb_], b_WmT], [bb[bs_]])
                        STT(t2t[:, 0:TG2].rearrange("p (a b) -> p a b", a=NB), banks[bs_][:, 0:TG2].rearrange("p (a b) -> p a b", a=NB),
                            cols[:, 16 + uc:17 + uc], bias2[:, uc, :].unsqueeze(1).broadcast_to([128, NB, 128]), ALU.mult, ALU.add,
                            [bb[bs_], b_cols, b_bias2], [b_t2])
                        TT("pool", yT[yk][:, ucl, :], t2t[:, 0:TG2], ugt[k][:, 0:TG2], ALU.mult, [b_t2, b_ug[k]], [b_yT[yk]])
                    for til in range(NB):
                        ti = tib + til
                        for hf in range(2):
                            rot_c[0] += 1
                            bk = rot_c[0] % 3
                            for ucl in range(4):
                                MM(banks[bk][:, :], yT[yk][:, ucl, til * 128:(til + 1) * 128], wob_sb[slo][:, ucl, hf * 512:(hf + 1) * 512],
                                   ucl == 0, ucl == 3, [b_yT[yk], b_wob[slo]], [bb[bk]])
                            xs = xres[:, ti, hf * 512:(hf + 1) * 512]
                            if ub == 0:
                                STT(xs, xs, ALPHA, banks[bk][:, :], ALU.mult, ALU.add, [b_xres[ti], bb[bk]], [b_xres[ti]])
                            else:
                                TT("dve", xs, xs, banks[bk][:, :], ALU.add, [b_xres[ti], bb[bk]], [b_xres[ti]])
                        if ub == 3:
                            ln_step(ti)
        flush_fin()
        P.fence()
        ffn(1, True)
        P.fence()
        P.emit(nc, st)
    return nc


def _host_x(inp, b):
    x = np.asarray(inp["x"], np.float32)[b]
    return {"x_tm": np.ascontiguousarray(x), "x_fm": np.ascontiguousarray(x.T)}


def _host_inputs(inp):
    f = np.float32
    d = {}
    d["wqkv"] = np.ascontiguousarray(np.asarray(inp["attn_w_qkv"], f)[0])
    d["wo"] = np.ascontiguousarray(np.asarray(inp["attn_w_o"], f)[0])
    d["win"] = np.ascontiguousarray(np.asarray(inp["gmlp_w_in"], f)[0])
    d["wout"] = np.ascontiguousarray(np.asarray(inp["gmlp_w_out"], f)[0])
    for l in range(2):
        for nm, key in (("wg", "ffn_w_gate"), ("wu", "ffn_w_up")):
            w = np.asarray(inp[key], f)[l]
            d["%s%d" % (nm, l)] = np.ascontiguousarray(w.reshape(8, 128, 22, 128).transpose(2, 1, 0, 3).reshape(22, 128, D))
        d["wd%d" % l] = np.ascontiguousarray(np.asarray(inp["ffn_w_down"], f)[l])
    lamv = np.concatenate([np.asarray(inp["attn_lambda_q1"], f)[0], np.asarray(inp["attn_lambda_q2"], f)[0],
                           np.asarray(inp["attn_lambda_k1"], f)[0], np.asarray(inp["attn_lambda_k2"], f)[0]])
    d["lamv"] = np.ascontiguousarray(np.broadcast_to(lamv[None, :], (128, 256)))
    d["gsub"] = np.ascontiguousarray(np.broadcast_to(np.asarray(inp["attn_subln_g"], f)[0][None, :], (128, 128)))
    b_in = np.asarray(inp["gmlp_b_in"], f)[0]
    d["binv"] = np.ascontiguousarray(np.broadcast_to(b_in[None, GH:], (128, GH)))
    cols = np.concatenate([b_in[:GH].reshape(16, 128).T, np.asarray(inp["gmlp_ln_g"], f)[0].reshape(16, 128).T,
                           np.asarray(inp["gmlp_ln_b"], f)[0].reshape(16, 128).T], axis=1)
    d["cols"] = np.ascontiguousarray(cols)
    d["wsT"] = np.ascontiguousarray(np.asarray(inp["gmlp_w_s"], f)[0].transpose(2, 0, 1))
    d["bsb"] = np.ascontiguousarray(np.broadcast_to(np.asarray(inp["gmlp_b_s"], f)[0][None], (128, 8, 128)))
    lnp = []
    for l in range(2):
        lnp += [np.asarray(inp["ln_mix_g"], f)[l], np.asarray(inp["ln_mix_b"], f)[l],
                np.asarray(inp["ln_ffn_g"], f)[l], np.asarray(inp["ln_ffn_b"], f)[l]]
    d["lnp"] = np.ascontiguousarray(np.broadcast_to(np.stack(lnp)[:, None, :], (8, 128, D)))
    return d


def _const_inputs():
    d = {}
    d["ident"] = np.eye(128, dtype=np.float32).astype(ml_dtypes.bfloat16)
    kl = np.arange(128, dtype=np.float32)[:, None, None]
    j = np.arange(4, dtype=np.float32)[None, :, None]
    ql = np.arange(128, dtype=np.float32)[None, None, :]
    bt = 128.0 * j - ql + kl
    btall = np.zeros((128, 5, 4, 128), np.float32)
    btall[:, 0] = bt
    kl2 = kl[:, 0, :]
    ql2 = ql[0]
    for jd in range(4):
        v = bt.copy()
        dg = 128.0 * jd - np.abs(ql2 - kl2)
        masked = (kl2 >= 64) & (ql2 < 64)
        dg = np.where(masked, NEG_BIG, dg)
        v[:, jd, :] = dg
        btall[:, 1 + jd] = v
    d["btall"] = btall.reshape(128, 5, 512)
    s = np.arange(128)[:, None]
    t = np.arange(128)[None, :]
    d["trilT"] = (t >= s).astype(np.float32)
    return d


_NC_CACHE = {}


def kernel(**inputs):
    x = np.asarray(inputs["x"])
    B, S, _ = x.shape
    if S not in _NC_CACHE:
        _NC_CACHE[S] = build(S)
    nc = _NC_CACHE[S]
    shared = _host_inputs(inputs)
    shared.update(_const_inputs())
    in_maps = []
    for b in range(B):
        d = dict(shared)
        d.update(_host_x(inputs, b))
        in_maps.append(d)
    res = run_bass_kernel_spmd(nc, in_maps, core_ids=list(range(B)))
    out = np.stack([np.asarray(r["out"]) for r in res.results], axis=0)
    return out.astype(np.float32)
```
